# Optimizing a Trainium2 kernel written in Bass

```python
import jax, jax.numpy as jnp
from jax import lax
import numpy as np

D_MODEL = 1024
BATCH = 4
SEQ = 4096
DEPTH = 4

N_MIXERS = 2
N_RET_LAYERS = (DEPTH + 1) // 2
N_GDN_LAYERS = DEPTH // 2

RET_HEADS = 4
RET_HEAD_QK = D_MODEL // RET_HEADS
RET_HEAD_V = 2 * RET_HEAD_QK
RET_QK = RET_HEADS * RET_HEAD_QK
RET_V = RET_HEADS * RET_HEAD_V
RET_IN = 2 * RET_QK + 2 * RET_V
RET_CHUNK = 128
ROPE_THETA = 10000.0

GDN_HEAD_K = 128
GDN_HEAD_V = 128
GDN_K_HEADS = D_MODEL // GDN_HEAD_K
GDN_V_HEADS = 2 * GDN_K_HEADS
GDN_QK = GDN_K_HEADS * GDN_HEAD_K
GDN_V = GDN_V_HEADS * GDN_HEAD_V
GDN_CONV_DIM = 2 * GDN_QK + GDN_V
GDN_IN = GDN_CONV_DIM + GDN_V + 2 * GDN_V_HEADS
GDN_CONV_K = 4
GDN_CHUNK = 64

FFN_HIDDEN = -(-8 * D_MODEL // (3 * 256)) * 256
NORM_EPS = 1e-6

kernel_name = 'hybrid_retention_gated_deltanet_trunk'


def rmsnorm(x, gain):
    xf = x.astype(jnp.float32)
    y = xf * lax.rsqrt(jnp.mean(xf * xf, axis=-1, keepdims=True) + NORM_EPS)
    return (y * gain.astype(jnp.float32)).astype(x.dtype)


def rope(t, positions):
    half = t.shape[-1] // 2
    inv_freq = ROPE_THETA ** (-jnp.arange(half, dtype=jnp.float32) / half)
    ang = positions.astype(jnp.float32)[..., None] * inv_freq
    cos = jnp.cos(ang)[:, :, None, :]
    sin = jnp.sin(ang)[:, :, None, :]
    t1, t2 = t[..., :half], t[..., half:]
    return jnp.concatenate([t1 * cos - t2 * sin, t2 * cos + t1 * sin], axis=-1)


def causal_depthwise_conv(x, w):
    c = x.shape[-1]
    return lax.conv_general_dilated(
        x, w.astype(x.dtype)[:, None, :], window_strides=(1,),
        padding=[(GDN_CONV_K - 1, 0)], dimension_numbers=('NWC', 'WIO', 'NWC'),
        feature_group_count=c)


def to_chunks(t, c):
    b, s, h, d = t.shape
    return t.reshape(b, s // c, c, h, d).transpose(0, 3, 1, 2, 4)


def retention_chunked(q, k, v):
    b, s, h, dk = q.shape
    dv = v.shape[-1]
    c = RET_CHUNK
    log_gamma = jnp.log1p(-jnp.exp2(-5.0 - jnp.arange(h, dtype=jnp.float32)))
    qc, kc, vc = to_chunks(q, c), to_chunks(k, c), to_chunks(v, c)
    pos = jnp.arange(c, dtype=jnp.float32)
    causal = pos[:, None] >= pos[None, :]
    diff = jnp.where(causal, pos[:, None] - pos[None, :], 0.0)
    decay = jnp.where(causal, jnp.exp(log_gamma[:, None, None] * diff), 0.0)
    scores = jnp.einsum('bhncd,bhnmd->bhncm', qc, kc) * decay[None, :, None]
    inner = jnp.einsum('bhncm,bhnmv->bhncv', scores, vc)
    q_dec = qc * jnp.exp(log_gamma[:, None] * (pos + 1.0))[None, :, None, :, None]
    k_dec = kc * jnp.exp(log_gamma[:, None] * (c - 1.0 - pos))[None, :, None, :, None]
    chunk_decay = jnp.exp(log_gamma * c)[None, :, None, None]

    def step(state, xs):
        qd, kd, vi = xs
        o = jnp.einsum('bhcd,bhdv->bhcv', qd, state)
        state = state * chunk_decay + jnp.einsum('bhcd,bhcv->bhdv', kd, vi)
        return state, o

    xs = (jnp.moveaxis(q_dec, 2, 0), jnp.moveaxis(k_dec, 2, 0), jnp.moveaxis(vc, 2, 0))
    _, cross = lax.scan(step, jnp.zeros((b, h, dk, dv), jnp.float32), xs)
    out = inner + jnp.moveaxis(cross, 0, 2)
    return out.transpose(0, 2, 3, 1, 4).reshape(b, s, h, dv)


def gated_delta_chunked(q, k, v, g, beta):
    b, s, h, dk = q.shape
    dv = v.shape[-1]
    c = GDN_CHUNK
    n = s // c
    qc, kc, vc = to_chunks(q, c), to_chunks(k, c), to_chunks(v, c)
    gc = g.reshape(b, n, c, h).transpose(0, 3, 1, 2)
    bc = beta.reshape(b, n, c, h).transpose(0, 3, 1, 2)
    gcum = jnp.cumsum(gc, axis=-1)
    idx = jnp.arange(c)
    causal = idx[:, None] >= idx[None, :]
    strict = idx[:, None] > idx[None, :]
    gdiff = jnp.where(causal, gcum[..., :, None] - gcum[..., None, :], 0.0)
    decay_mat = jnp.where(causal, jnp.exp(gdiff), 0.0)
    k_beta = kc * bc[..., None]
    a_mat = jnp.where(strict, jnp.einsum('bhncd,bhnmd->bhncm', k_beta, kc) * decay_mat, 0.0)
    lhs = a_mat + jnp.eye(c, dtype=jnp.float32)
    rhs = jnp.concatenate([vc * bc[..., None], k_beta * jnp.exp(gcum)[..., None]], axis=-1)
    sol = lax.linalg.triangular_solve(lhs, rhs, left_side=True, lower=True, unit_diagonal=True)
    u = sol[..., :dv]
    w = sol[..., dv:]
    attn = jnp.where(causal, jnp.einsum('bhncd,bhnmd->bhncm', qc, kc) * decay_mat, 0.0)
    q_dec = qc * jnp.exp(gcum)[..., None]
    g_last = gcum[..., -1]
    k_dec = kc * jnp.exp(g_last[..., None] - gcum)[..., None]

    def step(state, xs):
        ui, wi, qi, ki, ai, gl = xs
        v_new = ui - jnp.einsum('bhcd,bhdv->bhcv', wi, state)
        o = jnp.einsum('bhcd,bhdv->bhcv', qi, state) + jnp.einsum('bhcm,bhmv->bhcv', ai, v_new)
        state = state * jnp.exp(gl)[..., None, None] + jnp.einsum('bhcd,bhcv->bhdv', ki, v_new)
        return state, o

    xs = tuple(jnp.moveaxis(t, 2, 0) for t in (u, w, q_dec, k_dec, attn, g_last))
    _, out = lax.scan(step, jnp.zeros((b, h, dk, dv), jnp.float32), xs)
    out = jnp.moveaxis(out, 0, 2)
    return out.transpose(0, 2, 3, 1, 4).reshape(b, s, h, dv)


def retention_mixer(hdn, positions, w_in, gn_gain, w_out):
    b, s, _ = hdn.shape
    proj = hdn @ w_in
    q, k, v, gate = jnp.split(proj, [RET_QK, 2 * RET_QK, 2 * RET_QK + RET_V], axis=-1)
    q = rope(q.reshape(b, s, RET_HEADS, RET_HEAD_QK).astype(jnp.float32), positions)
    k = rope(k.reshape(b, s, RET_HEADS, RET_HEAD_QK).astype(jnp.float32), positions) * (RET_HEAD_QK ** -0.5)
    v = v.reshape(b, s, RET_HEADS, RET_HEAD_V).astype(jnp.float32)
    o = retention_chunked(q, k, v)
    mu = jnp.mean(o, axis=-1, keepdims=True)
    var = jnp.mean(jnp.square(o - mu), axis=-1, keepdims=True)
    o = ((o - mu) * lax.rsqrt(var + NORM_EPS)).reshape(b, s, RET_V) * gn_gain.astype(jnp.float32)
    y = jax.nn.silu(gate.astype(jnp.float32)) * o
    return y.astype(hdn.dtype) @ w_out


def l2norm(t):
    return t * lax.rsqrt(jnp.sum(t * t, axis=-1, keepdims=True) + NORM_EPS)


def gdn_mixer(hdn, w_in, conv_w, a_log, dt_bias, norm_gain, w_out):
    b, s, _ = hdn.shape
    proj = hdn @ w_in
    qkv, z, beta_logit, a = jnp.split(
        proj, [GDN_CONV_DIM, GDN_CONV_DIM + GDN_V, GDN_CONV_DIM + GDN_V + GDN_V_HEADS], axis=-1)
    qkv = jax.nn.silu(causal_depthwise_conv(qkv, conv_w))
    q, k, v = jnp.split(qkv, [GDN_QK, 2 * GDN_QK], axis=-1)
    rep = GDN_V_HEADS // GDN_K_HEADS
    q = jnp.repeat(l2norm(q.reshape(b, s, GDN_K_HEADS, GDN_HEAD_K).astype(jnp.float32)), rep, axis=2)
    q = q * (GDN_HEAD_K ** -0.5)
    k = jnp.repeat(l2norm(k.reshape(b, s, GDN_K_HEADS, GDN_HEAD_K).astype(jnp.float32)), rep, axis=2)
    v = v.reshape(b, s, GDN_V_HEADS, GDN_HEAD_V).astype(jnp.float32)
    beta = jax.nn.sigmoid(beta_logit.astype(jnp.float32))
    g = -jnp.exp(a_log.astype(jnp.float32)) * jax.nn.softplus(
        a.astype(jnp.float32) + dt_bias.astype(jnp.float32))
    o = gated_delta_chunked(q, k, v, g, beta)
    o = o * lax.rsqrt(jnp.mean(o * o, axis=-1, keepdims=True) + NORM_EPS) * norm_gain.astype(jnp.float32)
    o = o.reshape(b, s, GDN_V) * jax.nn.silu(z.astype(jnp.float32))
    return o.astype(hdn.dtype) @ w_out


def swiglu(hdn, w_in, w_out):
    gate, up = jnp.split(hdn @ w_in, 2, axis=-1)
    return (jax.nn.silu(gate) * up) @ w_out


def setup_inputs(seed: int = 0) -> dict:
    key = jax.random.key(seed)
    ks = jax.random.split(key, 20)
    f32 = jnp.float32
    nrm = lambda k, shape, scale: jax.random.normal(k, shape, f32) * scale
    out_scale = 0.5
    x = jax.random.normal(ks[0], (BATCH, SEQ, D_MODEL), f32)
    positions = jnp.broadcast_to(jnp.arange(SEQ, dtype=jnp.int32), (BATCH, SEQ))
    norm_mix = 1.0 + nrm(ks[1], (DEPTH, D_MODEL), 0.02)
    norm_ffn = 1.0 + nrm(ks[2], (DEPTH, D_MODEL), 0.02)
    norm_final = 1.0 + nrm(ks[3], (D_MODEL,), 0.02)
    ret_w_in = nrm(ks[4], (N_RET_LAYERS, D_MODEL, RET_IN), D_MODEL ** -0.5)
    ret_gn_gain = 1.0 + nrm(ks[5], (N_RET_LAYERS, RET_V), 0.02)
    ret_w_out = nrm(ks[6], (N_RET_LAYERS, RET_V, D_MODEL), out_scale * RET_V ** -0.5)
    gdn_w_in = nrm(ks[7], (N_GDN_LAYERS, D_MODEL, GDN_IN), D_MODEL ** -0.5)
    gdn_conv = nrm(ks[8], (N_GDN_LAYERS, GDN_CONV_K, GDN_CONV_DIM), GDN_CONV_K ** -0.5)
    gdn_a_log = jnp.log(jax.random.uniform(ks[9], (N_GDN_LAYERS, GDN_V_HEADS), f32, 1.0, 16.0))
    dt = jnp.exp(jax.random.uniform(ks[10], (N_GDN_LAYERS, GDN_V_HEADS), f32,
                                    float(np.log(1e-3)), float(np.log(1e-1))))
    gdn_dt_bias = dt + jnp.log(-jnp.expm1(-dt))
    gdn_norm_gain = 1.0 + nrm(ks[11], (N_GDN_LAYERS, GDN_HEAD_V), 0.02)
    gdn_w_out = nrm(ks[12], (N_GDN_LAYERS, GDN_V, D_MODEL), out_scale * GDN_V ** -0.5)
    ffn_w_in = nrm(ks[13], (DEPTH, D_MODEL, 2 * FFN_HIDDEN), D_MODEL ** -0.5)
    ffn_w_out = nrm(ks[14], (DEPTH, FFN_HIDDEN, D_MODEL), out_scale * FFN_HIDDEN ** -0.5)
    return {'x': x, 'positions': positions, 'norm_mix': norm_mix, 'norm_ffn': norm_ffn,
            'norm_final': norm_final, 'ret_w_in': ret_w_in, 'ret_gn_gain': ret_gn_gain,
            'ret_w_out': ret_w_out, 'gdn_w_in': gdn_w_in, 'gdn_conv': gdn_conv,
            'gdn_a_log': gdn_a_log, 'gdn_dt_bias': gdn_dt_bias, 'gdn_norm_gain': gdn_norm_gain,
            'gdn_w_out': gdn_w_out, 'ffn_w_in': ffn_w_in, 'ffn_w_out': ffn_w_out}


def reference(x, positions, norm_mix, norm_ffn, norm_final, ret_w_in, ret_gn_gain, ret_w_out,
              gdn_w_in, gdn_conv, gdn_a_log, gdn_dt_bias, gdn_norm_gain, gdn_w_out,
              ffn_w_in, ffn_w_out):
    h = x
    for i in range(DEPTH):
        j = i // N_MIXERS
        hn = rmsnorm(h, norm_mix[i])
        if i % N_MIXERS == 0:
            mix = retention_mixer(hn, positions, ret_w_in[j], ret_gn_gain[j], ret_w_out[j])
        else:
            mix = gdn_mixer(hn, gdn_w_in[j], gdn_conv[j], gdn_a_log[j], gdn_dt_bias[j],
                            gdn_norm_gain[j], gdn_w_out[j])
        h = h + mix
        h = h + swiglu(rmsnorm(h, norm_ffn[i]), ffn_w_in[i], ffn_w_out[i])
    return rmsnorm(h, norm_final)
```

```python
import numpy as np
from contextlib import ExitStack
import concourse.bass as bass
import concourse.mybir as mybir
from concourse.bass_utils import run_bass_kernel_spmd

F32 = mybir.dt.float32
BF16 = mybir.dt.bfloat16
I32 = mybir.dt.int32
AF = mybir.ActivationFunctionType
ALU = mybir.AluOpType

D = 1024
SEQ = 4096
BATCH = 4
DEPTH = 4
TT = 512
NSUB = 4
FFH = 2816
NHB = 22
EPS = 1e-6
GAM = [1.0 - 2.0 ** (-5 - h) for h in range(4)]

O_NM, O_NF, O_NFIN, O_RGN, O_GNG, O_CW, O_ALOG, O_DTB, O_INVF, O_GK = 0, 32, 64, 72, 104, 106, 362, 394, 426, 427
NPRM = 431
C_ID, C_U, C_SL, C_MLI, C_ONES, C_DECT, C_GQ, C_M32, C_M64, C_M128 = 0, 128, 256, 384, 512, 640, 1152, 1664, 1792, 1920
NCM = 2048

SLOTS_RET = 16 + 19
SLOTS_GDN = 17 + 19
NSLOT = 2 * SLOTS_RET + 2 * SLOTS_GDN

SEM_LIM = 30000
ENGS = ("pe", "act", "dve", "pool", "sp")


def _slot_in(W, cols):
    n = len(cols)
    s = np.zeros((128, 4096), np.float32)
    s[:, :8 * n] = W[:, cols].reshape(8, 128, n).transpose(1, 0, 2).reshape(128, 8 * n)
    return s


def _slot_out_mixer(W, r0):
    return W[r0:r0 + 512].reshape(4, 128, 1024).transpose(1, 0, 2).reshape(128, 4096)


def _slot_out_ffn(W, d):
    s = np.zeros((128, 4096), np.float32)
    s[:, :NHB * 128] = W[:, d * 128:(d + 1) * 128].reshape(NHB, 128, 128).transpose(1, 0, 2).reshape(128, NHB * 128)
    return s


def _ffn_slots(w_in, w_out):
    out = []
    for s in range(11):
        cols = []
        for hb in (2 * s, 2 * s + 1):
            cols += list(range(hb * 128, hb * 128 + 128))
            cols += list(range(FFH + hb * 128, FFH + hb * 128 + 128))
        out.append(_slot_in(w_in, np.array(cols)))
    for d in range(8):
        out.append(_slot_out_ffn(w_out, d))
    return out


def pack_weights(inp):
    slots = []
    ar = np.arange
    for l in range(DEPTH):
        j = l // 2
        if l % 2 == 0:
            wi, wo = inp["ret_w_in"][j], inp["ret_w_out"][j]
            for h in range(4):
                slots.append(_slot_in(wi, np.concatenate([h * 256 + ar(256), 1024 + h * 256 + ar(256)])))
                slots.append(_slot_in(wi, 2048 + h * 512 + ar(512)))
                slots.append(_slot_in(wi, 4096 + h * 512 + ar(512)))
                slots.append(_slot_out_mixer(wo, h * 512))
        else:
            wi, wo = inp["gdn_w_in"][j], inp["gdn_w_out"][j]
            slots.append(_slot_in(wi, 6144 + ar(32)))
            for g in range(4):
                slots.append(_slot_in(wi, np.concatenate([g * 256 + ar(256), 1024 + g * 256 + ar(256)])))
                slots.append(_slot_in(wi, 2048 + g * 512 + ar(512)))
                slots.append(_slot_in(wi, 4096 + g * 512 + ar(512)))
                slots.append(_slot_out_mixer(wo, g * 512))
        slots += _ffn_slots(inp["ffn_w_in"][l], inp["ffn_w_out"][l])
    assert len(slots) == NSLOT
    return np.ascontiguousarray(np.stack(slots, 0))


def pack_params(inp):
    p = np.zeros((128, NPRM), np.float32)
    fm = lambda v: np.asarray(v, np.float32).reshape(-1, 128).T
    for l in range(4):
        p[:, O_NM + 8 * l:O_NM + 8 * l + 8] = fm(inp["norm_mix"][l])
        p[:, O_NF + 8 * l:O_NF + 8 * l + 8] = fm(inp["norm_ffn"][l])
    p[:, O_NFIN:O_NFIN + 8] = fm(inp["norm_final"])
    for j in range(2):
        p[:, O_RGN + 16 * j:O_RGN + 16 * j + 16] = fm(inp["ret_gn_gain"][j])
        p[:, O_GNG + j] = np.asarray(inp["gdn_norm_gain"][j], np.float32)
        cw = np.asarray(inp["gdn_conv"][j], np.float32)
        p[:, O_CW + 128 * j:O_CW + 128 * j + 128] = cw.reshape(4, 32, 128).transpose(2, 1, 0).reshape(128, 128)
        p[:, O_ALOG + 16 * j:O_ALOG + 16 * j + 16] = np.asarray(inp["gdn_a_log"][j], np.float32)[None, :]
        p[:, O_DTB + 16 * j:O_DTB + 16 * j + 16] = np.asarray(inp["gdn_dt_bias"][j], np.float32)[None, :]
    i = np.arange(128, dtype=np.float64)
    p[:, O_INVF] = (10000.0 ** (-i / 128.0) / (2 * np.pi)).astype(np.float32)
    for h in range(4):
        p[:, O_GK + h] = (GAM[h] ** (127.0 - i)).astype(np.float32)
    return p


def const_mats():
    c = np.zeros((128, NCM), np.float32)
    r = np.arange(128)[:, None].astype(np.float64)
    q = np.arange(128)[None, :].astype(np.float64)
    c[:, C_ID:C_ID + 128] = (r == q)
    c[:, C_U:C_U + 128] = (r <= q)
    c[:, C_SL:C_SL + 128] = (r > q)
    c[:, C_MLI:C_MLI + 128] = (r >= q)
    c[:, C_ONES:C_ONES + 128] = 1.0
    c[:, C_M32:C_M32 + 128] = (r // 32 == q // 32)
    c[:, C_M64:C_M64 + 128] = (r // 64 == q // 64) & (r // 32 != q // 32)
    c[:, C_M128:C_M128 + 128] = (r // 64 != q // 64)
    for h in range(4):
        c[:, C_DECT + 128 * h:C_DECT + 128 * h + 128] = np.where(q >= r, GAM[h] ** np.maximum(q - r, 0), 0.0)
        c[:, C_GQ + 128 * h:C_GQ + 128 * h + 128] = np.broadcast_to(GAM[h] ** (q + 1.0), (128, 128))
    return c


class View:
    small = False

    def __init__(self, arena, lo, n):
        self.arena, self.lo, self.n = arena, lo, n

    def ap(self):
        return self.arena.t[:, self.lo:self.lo + self.n]

    def r(self, pat, **kw):
        return self.ap().rearrange(pat, **kw)

    def sub(self, off, n):
        assert off + n <= self.n
        return View(self.arena, self.lo + off, n)

    def keys(self):
        g = self.arena.G
        return [(self.arena.name, i) for i in range(self.lo // g, (self.lo + self.n - 1) // g + 1)]


class Arena:
    def __init__(self, name, t, G):
        self.name, self.t, self.G, self.ptr = name, t, G, 0

    def alloc(self, n, small=False):
        lo = (self.ptr + self.G - 1) // self.G * self.G
        self.ptr = lo + n
        assert self.ptr <= self.t.shape[1], (self.name, self.ptr, self.t.shape)
        v = View(self, lo, n)
        v.small = small
        return v

    def reset(self, to=0):
        self.ptr = to


def _has_small(items):
    for it in items:
        if isinstance(it, View):
            if it.small:
                return True
        elif isinstance(it, list) and _has_small(it):
            return True
    return False


def _keys(items):
    out = []
    for it in items:
        if isinstance(it, View):
            out.extend(it.keys())
        elif isinstance(it, list) or (isinstance(it, tuple) and not isinstance(it[0], str)):
            out.extend(_keys(it))
        else:
            out.append(it)
    return out


class Op:
    __slots__ = ("eng", "fn", "waits", "signal", "dma_key", "dma_val", "idx")


class Prog:
    def __init__(self):
        self.ops = {e: [] for e in ENGS}
        self.last_w, self.readers = {}, {}
        self.known = {e: {} for e in ENGS}
        self.dma_cnt = {}

    def op(self, eng, fn, reads=(), writes=(), dma=None):
        o = Op()
        o.eng, o.fn, o.signal, o.dma_key, o.dma_val = eng, fn, False, dma, 0
        lst = self.ops[eng]
        lst.append(o)
        o.idx = len(lst)
        rk, wk = _keys(reads), _keys(writes)
        need = {}
        known = self.known[eng]
        ss = _has_small(reads)

        def want(tok, raw=False):
            sk, val = tok
            if (sk == eng and eng == "pe") or known.get(sk, 0) >= val:
                return
            if need.get(sk, 0) < val:
                need[sk] = val

        for k in rk:
            t = self.last_w.get(k)
            if t is not None:
                want(t, True)
        for k in wk:
            t = self.last_w.get(k)
            if t is not None:
                want(t)
            for t in self.readers.get(k, ()):
                want(t)
        if dma is not None and self.dma_cnt.get(dma, 0) > 0:
            want(("dma:" + dma, self.dma_cnt[dma]))
        for sk, val in need.items():
            known[sk] = val
        o.waits = list(need.items())
        if dma is not None:
            c = self.dma_cnt.get(dma, 0) + 16
            self.dma_cnt[dma] = c
            o.dma_val = c
            tok = ("dma:" + dma, c)
        else:
            tok = (eng, o.idx)
        for k in rk:
            lst = self.readers.setdefault(k, [])
            lst[:] = [t for t in lst if t[0] != tok[0]]
            lst.append(tok)
        for k in wk:
            self.last_w[k] = tok
            self.readers[k] = []
        return o

    def emit(self, nc, es):
        for e in ENGS:
            for o in self.ops[e]:
                for sk, val in o.waits:
                    if not sk.startswith("dma:"):
                        self.ops[sk][val - 1].signal = True
        sig = {}
        for e in ENGS:
            n, l = 0, []
            for o in self.ops[e]:
                n += 1 if o.signal else 0
                l.append(n)
            sig[e] = l
        sems = {}
        for e in ENGS:
            tot = sig[e][-1] if sig[e] else 0
            for ep in range((tot + SEM_LIM - 1) // SEM_LIM):
                sems[(e, ep)] = es.enter_context(nc.semaphore(f"s_{e}_{ep}"))
        for k in self.dma_cnt:
            assert self.dma_cnt[k] < 60000, (k, self.dma_cnt[k])
            sems["dma:" + k] = es.enter_context(nc.semaphore("d_" + k))
        block = es.enter_context(nc.Block())

        def run(en):
            def body(eng):
                for o in self.ops[en]:
                    for sk, val in o.waits:
                        if sk.startswith("dma:"):
                            eng.wait_ge(sems[sk], val)
                        else:
                            s = sig[sk][val - 1]
                            eng.wait_ge(sems[(sk, (s - 1) // SEM_LIM)], (s - 1) % SEM_LIM + 1)
                    if o.fn is None:
                        continue
                    ins = o.fn(eng)
                    if o.dma_key is not None:
                        ins.then_inc(sems["dma:" + o.dma_key], 16)
                    elif o.signal:
                        s = sig[en][o.idx - 1]
                        ins.then_inc(sems[(en, (s - 1) // SEM_LIM)], 1)
            return body

        block.tensor(run("pe"))
        block.scalar(run("act"))
        block.vector(run("dve"))
        block.gpsimd(run("pool"))
        block.sync(run("sp"))


def build_program(n_tiles=SEQ // TT, layers=(0, 1, 2, 3), final_norm=True, debug=False, nslot=NSLOT):
    nc = bass.Bass("TRN2", target_bir_lowering=False)
    NSLOT_ = nslot
    xT = nc.dram_tensor("xT", [D, SEQ], F32, kind="ExternalInput").ap()
    pos = nc.dram_tensor("pos", [1, SEQ], I32, kind="ExternalInput").ap()
    prm_d = nc.dram_tensor("prm", [128, NPRM], F32, kind="ExternalInput").ap()
    cm_d = nc.dram_tensor("cmat", [128, NCM], F32, kind="ExternalInput").ap()
    wall = nc.dram_tensor("wall", [NSLOT_, 128, 4096], F32, kind="ExternalInput").ap()
    dbg = nc.dram_tensor("dbg", [32, 128, 4096], F32, kind="ExternalOutput").ap() if debug else None
    dbgb = nc.dram_tensor("dbgb", [32, 128, 4096], BF16, kind="ExternalOutput").ap() if debug else None
    outT = nc.dram_tensor("outT", [D, SEQ], F32, kind="ExternalOutput").ap()
    wsc = nc.dram_tensor("wsc", [NSLOT_, 128, 4096], BF16, kind="Internal").ap()

    P = Prog()
    with ExitStack() as es:
        sb = lambda n, s, d: es.enter_context(nc.sbuf_tensor(n, s, d))
        hT = sb("hT", [128, 8, TT], F32)
        hnT = sb("hnT", [128, 8, TT], BF16)
        wsl = sb("wsl", [128, 3, 4096], BF16)
        retS = sb("retS", [128, 2, 4, 2, 512], F32)
        gdnS = sb("gdnS", [128, 2, 16, 128], F32)
        hist = sb("hist", [128, 2, 32, 3], F32)
        prm = sb("prm_sb", [128, NPRM], F32)
        cm = sb("cm_sb", [128, NCM], F32)
        identb = sb("identb", [128, 128], BF16)
        onesb = sb("onesb", [128, 128], BF16)
        nexpA = sb("nexpA", [128, 2, 16], F32)
        posi = sb("posi", [128, TT], I32)
        yi = sb("yi", [128, TT], I32)
        BBt = sb("BBt", [128, 12 * 2048], BF16)
        FFt = sb("FFt", [128, 23 * 512], F32)
        BB = Arena("BB", BBt, 512)
        FF = Arena("FF", FFt, 512)
        psb = [es.enter_context(nc.psum_tensor(f"ps{i}", [128, 512], F32)) for i in range(8)]
        bank_ctr = [0]

        def bank():
            b = bank_ctr[0] % 8
            bank_ctr[0] += 1
            return b

        ps = lambda b: psb[b][:, :]
        psh = lambda b: psb[b][:, :].bitcast(BF16)
        PK = lambda b: ("ps", b)
        cmv = lambda off, n=128: cm[:, off:off + n]
        prc = lambda off, n=1: prm[:, off:off + n]

        P.op("sp", lambda e: e.dma_start(out=prm[:, :], in_=prm_d[:, :]), writes=["prm"], dma="ldprm")
        P.op("sp", lambda e: e.dma_start(out=cm[:, :], in_=cm_d[:, :]), writes=["cm"], dma="ldcm")
        for s in range(NSLOT_):
            P.op("pool", lambda e, s=s: e.dma_start(out=wsc[s], in_=wall[s]), writes=[("wsc", s)], dma=f"cv{s % 8}")
        P.op("dve", lambda e: e.tensor_copy(out=identb[:, :], in_=cmv(C_ID)), reads=["cm"], writes=["identb"])
        P.op("dve", lambda e: e.tensor_copy(out=onesb[:, :], in_=cmv(C_ONES)), reads=["cm"], writes=["onesb"])
        P.op("dve", lambda e: e.memset(retS[:, :, :, :, :], 0.0), writes=["retS"])
        P.op("dve", lambda e: e.memset(gdnS[:, :, :, :], 0.0), writes=["gdnS"])
        P.op("dve", lambda e: e.memset(hist[:, :, :, :], 0.0), writes=["hist"])
        P.op("act", lambda e: e.activation(out=nexpA[:, :, :], in_=prm[:, O_ALOG:O_ALOG + 32].rearrange("p (j h) -> p j h", j=2), func=AF.Exp),
             reads=["prm"], writes=["nexpA"])
        P.op("dve", lambda e: e.tensor_scalar(out=nexpA[:, :, :], in0=nexpA[:, :, :], scalar1=-1.0, scalar2=None, op0=ALU.mult),
             reads=["nexpA"], writes=["nexpA"])

        dcount = {}

        def dump(idx, ap, n, reads):
            if not debug or dcount.get(idx):
                return
            dcount[idx] = 1
            dst = dbgb if ap.dtype == BF16 else dbg
            P.op("sp", lambda e: e.dma_start(out=dst[idx][:, 0:n], in_=ap), reads=reads, writes=[("dbg", idx)], dma=f"dbg{idx}")

        stream = [s for _ in range(n_tiles) for l in layers for s in _layer_slots(l)]
        st = {"n": 0, "issued": 0}

        def ws_next():
            n = st["n"]
            st["n"] += 1
            while st["issued"] < min(n + 3, len(stream)):
                m = st["issued"]
                s, j = stream[m], m % 3
                P.op("sp", lambda e, s=s, j=j: e.dma_start(out=wsl[:, j, :], in_=wsc[s]),
                     reads=[("wsc", s)], writes=[("wsl", j)], dma=f"w{j}")
                st["issued"] += 1
            return n % 3

        WK = lambda j: ("wsl", j)
        HT = [("hT", c) for c in range(8)]
        HN = [("hnT", c) for c in range(8)]

        def rmsnorm(goff, dst_f32=None):
            for half in range(2):
                cs = slice(half * 4, half * 4 + 4)
                P.op("act", lambda e, cs=cs: e.activation(out=hnT[:, cs, :], in_=hT[:, cs, :], func=AF.Square),
                     reads=HT[cs], writes=HN[cs])
            b = bank()
            for c in range(8):
                P.op("pe", lambda e, c=c, b=b: e.matmul(ps(b), lhsT=onesb[:, :], rhs=hnT[:, c, :], start=(c == 0), stop=(c == 7)),
                     reads=[("hnT", c), "onesb"], writes=[PK(b)])
            rs = FF.alloc(512)
            P.op("act", lambda e, b=b, rs=rs: e.activation(out=rs.ap(), in_=ps(b), func=AF.Sqrt, bias=EPS, scale=1.0 / D),
                 reads=[PK(b)], writes=[rs])
            P.op("dve", lambda e, rs=rs: e.reciprocal(out=rs.ap(), in_=rs.ap()), reads=[rs], writes=[rs])
            for c in range(8):
                en = "dve"
                if dst_f32 is None:
                    P.op(en, lambda e, c=c, rs=rs: e.scalar_tensor_tensor(out=hnT[:, c, :], in0=hT[:, c, :], scalar=prc(goff + c), in1=rs.ap(),
                                                                          op0=ALU.mult, op1=ALU.mult),
                         reads=[("hT", c), rs, "prm"], writes=[("hnT", c)])
                else:
                    dv = dst_f32.sub(c * 512, 512)
                    P.op(en, lambda e, c=c, rs=rs, dv=dv: e.scalar_tensor_tensor(out=dv.ap(), in0=hT[:, c, :], scalar=prc(goff + c), in1=rs.ap(),
                                                                                op0=ALU.mult, op1=ALU.mult),
                         reads=[("hT", c), rs, "prm"], writes=[dv])

        def proj_fm(j, blk, b):
            for kc in range(8):
                P.op("pe", lambda e, kc=kc: e.matmul(ps(b), lhsT=wsl[:, j, kc * 512 + blk * 128:kc * 512 + blk * 128 + 128], rhs=hnT[:, kc, :],
                                                    start=(kc == 0), stop=(kc == 7)),
                     reads=[WK(j), ("hnT", kc)], writes=[PK(b)])

        def out_proj(j, yv, nck, stride):
            for d in range(8):
                b = bank()
                for c in range(nck):
                    off = c * stride + d * 128
                    P.op("pe", lambda e, c=c, off=off, b=b: e.matmul(ps(b), lhsT=wsl[:, j, off:off + 128], rhs=yv.sub(c * 512, 512).ap(),
                                                                     start=(c == 0), stop=(c == nck - 1)),
                         reads=[WK(j), yv.sub(c * 512, 512)], writes=[PK(b)])
                P.op("dve", lambda e, d=d, b=b: e.tensor_tensor(out=hT[:, d, :], in0=hT[:, d, :], in1=ps(b), op=ALU.add),
                     reads=[("hT", d), PK(b)], writes=[("hT", d)])

        def ffn(l):
            FF.reset()
            BB.reset()
            rmsnorm(O_NF + 8 * l)
            act = BB.alloc(NHB * 512)
            sgs = [FF.alloc(512) for _ in range(4)]
            for s in range(11):
                j = ws_next()
                for w in range(2):
                    hb = 2 * s + w
                    bg, bu = bank(), bank()
                    proj_fm(j, 2 * w, bg)
                    proj_fm(j, 2 * w + 1, bu)
                    sg = sgs[(2 * s + w) % 4]
                    av = act.sub(hb * 512, 512)
                    P.op("act", lambda e, bg=bg, sg=sg: e.activation(out=sg.ap(), in_=ps(bg), func=AF.Silu), reads=[PK(bg)], writes=[sg])
                    P.op("dve", lambda e, bu=bu, sg=sg, av=av: e.tensor_tensor(out=av.ap(), in0=sg.ap(), in1=ps(bu), op=ALU.mult),
                         reads=[sg, PK(bu)], writes=[av])
            for d in range(8):
                j = ws_next()
                b = bank()
                for c in range(NHB):
                    P.op("pe", lambda e, c=c, b=b, j=j: e.matmul(ps(b), lhsT=wsl[:, j, c * 128:c * 128 + 128], rhs=act.sub(c * 512, 512).ap(),
                                                                 start=(c == 0), stop=(c == NHB - 1)),
                         reads=[WK(j), act.sub(c * 512, 512)], writes=[PK(b)])
                P.op("dve", lambda e, d=d, b=b: e.tensor_tensor(out=hT[:, d, :], in0=hT[:, d, :], in1=ps(b), op=ALU.add),
                     reads=[("hT", d), PK(b)], writes=[("hT", d)])

        def ret_layer(l):
            jl = l // 2
            FF.reset()
            BB.reset()
            rmsnorm(O_NM + 8 * l)
            cosT, sinT, cosq, sinq = [FF.alloc(512) for _ in range(4)]
            ff_base = FF.ptr
            yv, fr = FF.alloc(512), FF.alloc(512)
            dump(0, hnT[:, :, :].rearrange("p c t -> p (c t)"), 4096, HN)
            P.op("dve", lambda e: e.tensor_copy(out=yv.ap(), in_=posi[:, :]), reads=["posi"], writes=[yv])
            P.op("dve", lambda e: e.tensor_scalar(out=yv.ap(), in0=yv.ap(), scalar1=prc(O_INVF), scalar2=None, op0=ALU.mult),
                 reads=[yv, "prm"], writes=[yv])
            P.op("dve", lambda e: e.tensor_copy(out=yi[:, :], in_=yv.ap()), reads=[yv], writes=["yi"])
            P.op("dve", lambda e: e.tensor_copy(out=fr.ap(), in_=yi[:, :]), reads=["yi"], writes=[fr])
            P.op("dve", lambda e: e.tensor_tensor(out=yv.ap(), in0=yv.ap(), in1=fr.ap(), op=ALU.subtract), reads=[yv, fr], writes=[yv])
            TWO_PI = 6.28318
            P.op("act", lambda e: e.activation(out=sinT.ap(), in_=yv.ap(), func=AF.Sin, scale=TWO_PI), reads=[yv], writes=[sinT])
            P.op("dve", lambda e: e.tensor_scalar(out=yv.ap(), in0=yv.ap(), scalar1=0.25, scalar2=None, op0=ALU.add), reads=[yv], writes=[yv])
            P.op("dve", lambda e: e.tensor_scalar(out=fr.ap(), in0=yv.ap(), scalar1=0.5, scalar2=None, op0=ALU.is_gt), reads=[yv], writes=[fr])
            P.op("dve", lambda e: e.tensor_tensor(out=yv.ap(), in0=yv.ap(), in1=fr.ap(), op=ALU.subtract), reads=[yv, fr], writes=[yv])
            P.op("act", lambda e: e.activation(out=cosT.ap(), in_=yv.ap(), func=AF.Sin, scale=TWO_PI), reads=[yv], writes=[cosT])
            P.op("act", lambda e: e.mul(out=cosq.ap(), in_=cosT.ap(), mul=1.0 / 16), reads=[cosT], writes=[cosq])
            P.op("act", lambda e: e.mul(out=sinq.ap(), in_=sinT.ap(), mul=1.0 / 16), reads=[sinT], writes=[sinq])
            bb_base = BB.ptr
            dump(1, cosT.ap(), 512, [cosT])
            dump(2, sinT.ap(), 512, [sinT])
            def do_head(h):
                FF.reset(ff_base)
                BB.reset(bb_base)
                qk = BB.alloc(2048)
                kd = BB.alloc(1024)
                vsb = BB.alloc(2048)
                sg = BB.alloc(2048)
                on = BB.alloc(2048)
                sT = BB.alloc(512)
                oraw = FF.alloc(2048)
                stt = FF.alloc(512, small=True)
                junk = FF.alloc(512)
                qd = [FF.alloc(256) for _ in range(2)]
                rsh = [FF.alloc(512) for _ in range(4)]
                rt = [[FF.alloc(512), FF.alloc(512)] + rsh for _ in range(2)]
                j = ws_next()
                for pair in range(2):
                    b1, b2 = bank(), bank()
                    proj_fm(j, 2 * pair, b1)
                    proj_fm(j, 2 * pair + 1, b2)
                    s1, s2, ta, tb, tc, td = rt[pair]
                    ct, sn = (cosq, sinq) if pair == 0 else (cosT, sinT)
                    o1, o2 = qk.sub(2 * pair * 512, 512), qk.sub((2 * pair + 1) * 512, 512)
                    P.op("act", lambda e, b1=b1, s1=s1: e.activation(out=s1.ap(), in_=ps(b1), func=AF.Copy), reads=[PK(b1)], writes=[s1])
                    P.op("act", lambda e, b2=b2, s2=s2: e.activation(out=s2.ap(), in_=ps(b2), func=AF.Copy), reads=[PK(b2)], writes=[s2])
                    P.op("dve", lambda e, s1=s1, ct=ct, ta=ta: e.tensor_tensor(out=ta.ap(), in0=s1.ap(), in1=ct.ap(), op=ALU.mult), reads=[s1, ct], writes=[ta])
                    P.op("dve", lambda e, s2=s2, sn=sn, tb=tb: e.tensor_tensor(out=tb.ap(), in0=s2.ap(), in1=sn.ap(), op=ALU.mult), reads=[s2, sn], writes=[tb])
                    P.op("pool", lambda e, s2=s2, ct=ct, tc=tc: e.tensor_tensor(out=tc.ap(), in0=s2.ap(), in1=ct.ap(), op=ALU.mult), reads=[s2, ct], writes=[tc])
                    P.op("pool", lambda e, s1=s1, sn=sn, td=td: e.tensor_tensor(out=td.ap(), in0=s1.ap(), in1=sn.ap(), op=ALU.mult), reads=[s1, sn], writes=[td])
                    P.op("dve", lambda e, ta=ta, tb=tb, o1=o1: e.tensor_tensor(out=o1.ap(), in0=ta.ap(), in1=tb.ap(), op=ALU.subtract), reads=[ta, tb], writes=[o1])
                    P.op("pool", lambda e, tc=tc, td=td, o2=o2: e.tensor_tensor(out=o2.ap(), in0=tc.ap(), in1=td.ap(), op=ALU.add), reads=[tc, td], writes=[o2])
                j = ws_next()
                for sub in range(4):
                    b = bank()
                    for kc in range(8):
                        P.op("pe", lambda e, kc=kc, b=b, sub=sub, j=j: e.matmul(ps(b), lhsT=hnT[:, kc, sub * 128:(sub + 1) * 128],
                                                                               rhs=wsl[:, j, kc * 512:(kc + 1) * 512], start=(kc == 0), stop=(kc == 7)),
                             reads=[WK(j), ("hnT", kc)], writes=[PK(b)])
                    dv = vsb.sub(sub * 512, 512)
                    if sub % 2 == 0:
                        P.op("act", lambda e, b=b, dv=dv: e.activation(out=dv.ap(), in_=ps(b), func=AF.Copy), reads=[PK(b)], writes=[dv])
                    else:
                        P.op("dve", lambda e, b=b, dv=dv: e.tensor_copy(out=dv.ap(), in_=ps(b)), reads=[PK(b)], writes=[dv])
                j = ws_next()
                for blk in range(4):
                    b = bank()
                    proj_fm(j, blk, b)
                    dv = sg.sub(blk * 512, 512)
                    P.op("act", lambda e, b=b, dv=dv: e.activation(out=dv.ap(), in_=ps(b), func=AF.Silu), reads=[PK(b)], writes=[dv])
                b = bank()
                for sub in range(4):
                    for dch in range(2):
                        src = qk.sub((2 + dch) * 512 + sub * 128, 128)
                        P.op("pe", lambda e, b=b, src=src, o=(sub * 2 + dch) * 128: e.transpose(psh(b)[:, o:o + 128], src.ap(), identb[:, :]),
                             reads=[src, "identb"], writes=[PK(b)])
                P.op("act", lambda e, b=b, h=h: e.activation(out=kd.ap(), in_=psh(b)[:, 0:1024], func=AF.Copy, scale=prc(O_GK + h)),
                     reads=[PK(b), "prm"], writes=[kd])
                dump(3, qk.ap(), 2048, [qk])
                dump(4, kd.ap(), 1024, [kd])
                dump(5, vsb.ap(), 2048, [vsb])
                dump(6, sg.ap(), 2048, [sg])
                SK = ("retS", jl, h)
                for sub in range(4):
                    cs = slice(sub * 128, sub * 128 + 128)
                    q1, q2 = qk.sub(sub * 128, 128), qk.sub(512 + sub * 128, 128)
                    k1, k2 = qk.sub(1024 + sub * 128, 128), qk.sub(1536 + sub * 128, 128)
                    b = bank()
                    P.op("pe", lambda e, b=b, k1=k1, q1=q1: e.matmul(ps(b)[:, 0:128], lhsT=k1.ap(), rhs=q1.ap(), start=True, stop=False),
                         reads=[k1, q1], writes=[PK(b)])
                    P.op("pe", lambda e, b=b, k2=k2, q2=q2: e.matmul(ps(b)[:, 0:128], lhsT=k2.ap(), rhs=q2.ap(), start=False, stop=True),
                         reads=[k2, q2], writes=[PK(b)])
                    sTv = sT.sub(sub * 128, 128)
                    P.op("dve", lambda e, b=b, sTv=sTv, h=h: e.tensor_tensor(out=sTv.ap(), in0=ps(b)[:, 0:128], in1=cmv(C_DECT + 128 * h), op=ALU.mult),
                         reads=[PK(b), "cm"], writes=[sTv])
                    qdv = qd[sub % 2]
                    P.op("pool", lambda e, qdv=qdv, sub=sub, h=h: e.tensor_tensor(
                        out=qdv.r("p (c i) -> p c i", c=2),
                        in0=qk.sub(0, 1024).r("p (c t) -> p c t", c=2)[:, :, sub * 128:(sub + 1) * 128],
                        in1=cmv(C_GQ + 128 * h).unsqueeze(1).to_broadcast([128, 2, 128]), op=ALU.mult),
                        reads=[qk.sub(sub * 128, 128), qk.sub(512 + sub * 128, 128), "cm"], writes=[qdv])
                    vv = vsb.sub(sub * 512, 512)
                    b2 = bank()
                    P.op("pe", lambda e, b2=b2, sTv=sTv, vv=vv: e.matmul(ps(b2), lhsT=sTv.ap(), rhs=vv.ap(), start=True, stop=False),
                         reads=[sTv, vv], writes=[PK(b2)])
                    for dch in range(2):
                        P.op("pe", lambda e, b2=b2, qdv=qdv, dch=dch, h=h: e.matmul(ps(b2), lhsT=qdv.ap()[:, dch * 128:(dch + 1) * 128],
                                                                                   rhs=retS[:, jl, h, dch, :], start=False, stop=(dch == 1)),
                             reads=[qdv, SK], writes=[PK(b2)])
                    for dch in range(2):
                        b3 = bank()
                        kdv = kd.sub(sub * 256 + dch * 128, 128)
                        P.op("pe", lambda e, b3=b3, kdv=kdv, vv=vv: e.matmul(ps(b3), lhsT=kdv.ap(), rhs=vv.ap(), start=True, stop=True),
                             reads=[kdv, vv], writes=[PK(b3)])
                        P.op("dve", lambda e, b3=b3, dch=dch, h=h: e.scalar_tensor_tensor(out=retS[:, jl, h, dch, :], in0=retS[:, jl, h, dch, :],
                                                                                         scalar=float(GAM[h] ** 128), in1=ps(b3), op0=ALU.mult, op1=ALU.add),
                             reads=[SK, PK(b3)], writes=[SK])
                    ov = oraw.sub(sub * 512, 512)
                    P.op("act", lambda e, b2=b2, ov=ov, sub=sub: e.activation(out=ov.ap(), in_=ps(b2), func=AF.Copy, accum_out=stt.ap()[:, sub:sub + 1]),
                         reads=[PK(b2)], writes=[ov, stt])
                    P.op("act", lambda e, ov=ov, sub=sub: e.activation(out=junk.ap(), in_=ov.ap(), func=AF.Square, accum_out=stt.ap()[:, 4 + sub:5 + sub]),
                         reads=[ov], writes=[junk, stt])
                S_ = stt.ap()
                P.op("dve", lambda e: e.tensor_scalar(out=S_[:, 8:12], in0=S_[:, 0:4], scalar1=1.0 / 512, scalar2=None, op0=ALU.mult), reads=[stt], writes=[stt])
                P.op("dve", lambda e: e.tensor_tensor(out=S_[:, 20:24], in0=S_[:, 8:12], in1=S_[:, 8:12], op=ALU.mult), reads=[stt], writes=[stt])
                P.op("dve", lambda e: e.scalar_tensor_tensor(out=S_[:, 12:16], in0=S_[:, 4:8], scalar=1.0 / 512, in1=S_[:, 20:24], op0=ALU.mult, op1=ALU.subtract),
                     reads=[stt], writes=[stt])
                P.op("dve", lambda e: e.tensor_scalar(out=S_[:, 12:16], in0=S_[:, 12:16], scalar1=0.0, scalar2=None, op0=ALU.max), reads=[stt], writes=[stt])
                P.op("act", lambda e: e.activation(out=S_[:, 12:16], in_=S_[:, 12:16], func=AF.Sqrt, bias=EPS, scale=1.0), reads=[stt], writes=[stt])
                P.op("dve", lambda e: e.reciprocal(out=S_[:, 12:16], in_=S_[:, 12:16]), reads=[stt], writes=[stt])
                P.op("dve", lambda e: e.scalar_tensor_tensor(out=S_[:, 16:20], in0=S_[:, 8:12], scalar=-1.0, in1=S_[:, 12:16], op0=ALU.mult, op1=ALU.mult),
                     reads=[stt], writes=[stt])
                for sub in range(4):
                    ov, nv = oraw.sub(sub * 512, 512), on.sub(sub * 512, 512)
                    en = "dve" if sub % 2 == 0 else "pool"
                    P.op(en, lambda e, ov=ov, nv=nv, sub=sub: e.tensor_scalar(out=nv.ap(), in0=ov.ap(), scalar1=S_[:, 12 + sub:13 + sub],
                                                                             scalar2=S_[:, 16 + sub:17 + sub], op0=ALU.mult, op1=ALU.add),
                         reads=[ov, stt], writes=[nv])
                dump(7, oraw.ap(), 2048, [oraw])
                dump(8, stt.ap(), 512, [stt])
                dump(9, on.ap(), 2048, [on])
                for vb in range(4):
                    b = bank()
                    for sub in range(4):
                        src = on.sub(sub * 512 + vb * 128, 128)
                        P.op("pe", lambda e, b=b, src=src, sub=sub: e.transpose(psh(b)[:, sub * 128:(sub + 1) * 128], src.ap(), identb[:, :]),
                             reads=[src, "identb"], writes=[PK(b)])
                    dv = sg.sub(vb * 512, 512)
                    P.op("dve", lambda e, b=b, dv=dv, c=O_RGN + 16 * jl + 4 * h + vb: e.scalar_tensor_tensor(
                        out=dv.ap(), in0=psh(b)[:, 0:512], scalar=prc(c), in1=dv.ap(), op0=ALU.mult, op1=ALU.mult),
                        reads=[PK(b), dv, "prm"], writes=[dv])
                dump(10, sg.ap(), 2048, [sg])
                j = ws_next()
                out_proj(j, sg, 4, 1024)
                dump(11, hT[:, :, :].rearrange("p c t -> p (c t)"), 4096, HT)

            for h in range(4):
                do_head(h)

        def gdn_layer(l):
            jl = l // 2
            FF.reset()
            BB.reset()
            sm = FF.alloc(1024, small=True)
            ff_base = FF.ptr
            rmsnorm(O_NM + 8 * l)
            fld = lambda i: sm.ap()[:, i * 64:(i + 1) * 64].rearrange("p (s h) -> p s h", s=4)
            if debug:
                P.op("pool", lambda e: e.memset(sm.ap(), 0.0), writes=[sm])
            BL, A_, BETA, NBETA, X_, AX, E_, L_, G_, GC, GL, EGC, EGL, EKD, BEGE, TMP = range(16)
            j = ws_next()
            b = bank()
            for sub in range(4):
                for kc in range(8):
                    P.op("pe", lambda e, b=b, sub=sub, kc=kc, j=j: e.matmul(ps(b)[:, sub * 32:(sub + 1) * 32], lhsT=hnT[:, kc, sub * 128:(sub + 1) * 128],
                                                                           rhs=wsl[:, j, kc * 32:(kc + 1) * 32], start=(kc == 0), stop=(kc == 7)),
                         reads=[WK(j), ("hnT", kc)], writes=[PK(b)])
            pv = lambda b, lo: ps(b)[:, 0:128].rearrange("p (s c) -> p s c", c=32)[:, :, lo:lo + 16]
            bc16 = lambda ap2: ap2.unsqueeze(1).to_broadcast([128, 4, 16])
            P.op("act", lambda e, b=b: e.activation(out=fld(BETA), in_=pv(b, 0), func=AF.Sigmoid), reads=[PK(b)], writes=[sm])
            P.op("dve", lambda e, b=b: e.tensor_tensor(out=fld(X_), in0=pv(b, 16), in1=bc16(prm[:, O_DTB + 16 * jl:O_DTB + 16 * jl + 16]), op=ALU.add),
                 reads=[PK(b), "prm"], writes=[sm])
            P.op("act", lambda e: e.activation(out=fld(AX), in_=fld(X_), func=AF.Abs), reads=[sm], writes=[sm])
            P.op("act", lambda e: e.activation(out=fld(E_), in_=fld(AX), func=AF.Exp, scale=-1.0), reads=[sm], writes=[sm])
            P.op("act", lambda e: e.activation(out=fld(L_), in_=fld(E_), func=AF.Ln, bias=1.0, scale=1.0), reads=[sm], writes=[sm])
            P.op("dve", lambda e: e.tensor_scalar(out=fld(TMP), in0=fld(X_), scalar1=0.0, scalar2=None, op0=ALU.max), reads=[sm], writes=[sm])
            P.op("dve", lambda e: e.tensor_tensor(out=fld(TMP), in0=fld(TMP), in1=fld(L_), op=ALU.add), reads=[sm], writes=[sm])
            P.op("dve", lambda e: e.tensor_tensor(out=fld(G_), in0=fld(TMP), in1=bc16(nexpA[:, jl, :]), op=ALU.mult), reads=[sm, "nexpA"], writes=[sm])
            b2 = bank()
            for sub in range(4):
                P.op("pe", lambda e, b2=b2, sub=sub: e.matmul(ps(b2)[:, sub * 32:sub * 32 + 16], lhsT=cmv(C_U), rhs=fld(G_)[:, sub, :], start=True, stop=True),
                     reads=[sm, "cm"], writes=[PK(b2)])
                P.op("pe", lambda e, b2=b2, sub=sub: e.matmul(ps(b2)[:, sub * 32 + 16:sub * 32 + 32], lhsT=cmv(C_ONES), rhs=fld(G_)[:, sub, :], start=True, stop=True),
                     reads=[sm, "cm"], writes=[PK(b2)])
            P.op("act", lambda e, b2=b2: e.activation(out=fld(GC), in_=pv(b2, 0), func=AF.Copy), reads=[PK(b2)], writes=[sm])
            P.op("act", lambda e, b2=b2: e.activation(out=fld(GL), in_=pv(b2, 16), func=AF.Copy), reads=[PK(b2)], writes=[sm])
            P.op("act", lambda e: e.activation(out=fld(EGC), in_=fld(GC), func=AF.Exp), reads=[sm], writes=[sm])
            P.op("act", lambda e: e.activation(out=fld(EGL), in_=fld(GL), func=AF.Exp), reads=[sm], writes=[sm])
            P.op("dve", lambda e: e.tensor_tensor(out=fld(TMP), in0=fld(GL), in1=fld(GC), op=ALU.subtract), reads=[sm], writes=[sm])
            P.op("act", lambda e: e.activation(out=fld(EKD), in_=fld(TMP), func=AF.Exp), reads=[sm], writes=[sm])
            P.op("dve", lambda e: e.tensor_tensor(out=fld(BEGE), in0=fld(BETA), in1=fld(EGC), op=ALU.mult), reads=[sm], writes=[sm])
            P.op("dve", lambda e: e.tensor_scalar(out=fld(NBETA), in0=fld(BETA), scalar1=-1.0, scalar2=None, op0=ALU.mult), reads=[sm], writes=[sm])
            col = lambda f, sub, hg: sm.ap()[:, f * 64 + sub * 16 + hg:f * 64 + sub * 16 + hg + 1]
            dump(12, sm.ap(), 1024, [sm])
            if GDN_STOP == 0:
                return

            bb_base = BB.ptr

            def do_group(g):
                FF.reset(ff_base)
                BB.reset(bb_base)
                qkT = BB.alloc(2048)
                vT = BB.alloc(2048)
                vtok = BB.alloc(2048)
                ktok = BB.alloc(1024)
                szT = BB.alloc(2048)
                sqb = BB.alloc(512)
                sq3 = [BB.alloc(512) for _ in range(3)]
                GQ = BB.alloc(1024)
                X1, X2, Pa, Pb, PTa, PTb, TTv, attnT, kbg, vbv, kdv = [BB.alloc(1024) for _ in range(11)]
                vnew = [BB.alloc(512) for _ in range(2)]
                xb2 = FF.alloc(1040)
                xbs = [xb2.sub(0, 520), xb2.sub(520, 520)]
                accs = [FF.alloc(512) for _ in range(2)]
                css = [FF.alloc(512) for _ in range(2)]
                rhsD = [FF.alloc(512) for _ in range(2)]
                nwT = [FF.alloc(512) for _ in range(2)]
                qTf = FF.alloc(1024)
                oraw = FF.alloc(2048)
                tmpo = FF.alloc(512)
                sj = FF.alloc(512, small=True)
                ssv, junk = sj.sub(0, 64), sj.sub(64, 128)
                ssv.small = True
                cnt = [0]

                def conv_block(b, bg, dst, sq=None):
                    i = cnt[0] % 2
                    cnt[0] += 1
                    xb, acc = xbs[i], accs[i]
                    HK = ("hist", jl, bg)
                    P.op("act", lambda e: e.activation(out=xb.ap()[:, 3:515], in_=ps(b), func=AF.Copy), reads=[PK(b)], writes=[xb])
                    P.op("pool", lambda e: e.tensor_copy(out=xb.ap()[:, 0:3], in_=hist[:, jl, bg, :]), reads=[HK], writes=[xb])
                    P.op("pool", lambda e: e.tensor_copy(out=hist[:, jl, bg, :], in_=xb.ap()[:, 512:515]), reads=[xb], writes=[HK])
                    cw = lambda k: prc(O_CW + 128 * jl + bg * 4 + k)
                    if sq is None:
                        P.op("pool", lambda e: e.tensor_scalar(out=acc.ap(), in0=xb.ap()[:, 3:515], scalar1=cw(3), scalar2=None, op0=ALU.mult),
                             reads=[xb, "prm"], writes=[acc])
                        for k in (2, 1, 0):
                            P.op("pool", lambda e, k=k: e.tensor_scalar(out=tmpo.ap(), in0=xb.ap()[:, k:k + 512], scalar1=cw(k), scalar2=None, op0=ALU.mult),
                                 reads=[xb, "prm"], writes=[tmpo])
                            P.op("pool", lambda e: e.tensor_tensor(out=acc.ap(), in0=acc.ap(), in1=tmpo.ap(), op=ALU.add), reads=[acc, tmpo], writes=[acc])
                    else:
                        P.op("dve", lambda e: e.tensor_scalar(out=acc.ap(), in0=xb.ap()[:, 3:515], scalar1=cw(3), scalar2=None, op0=ALU.mult),
                             reads=[xb, "prm"], writes=[acc])
                        for k in (2, 1, 0):
                            P.op("dve", lambda e, k=k: e.scalar_tensor_tensor(out=acc.ap(), in0=xb.ap()[:, k:k + 512], scalar=cw(k), in1=acc.ap(),
                                                                              op0=ALU.mult, op1=ALU.add),
                                 reads=[xb, acc, "prm"], writes=[acc])
                    P.op("act", lambda e: e.activation(out=dst.ap(), in_=acc.ap(), func=AF.Silu), reads=[acc], writes=[dst])
                    if sq is not None:
                        P.op("act", lambda e: e.activation(out=sq.ap(), in_=dst.ap(), func=AF.Square), reads=[dst], writes=[sq])

                def l2_finish(blk):
                    rn = css[blk % 2]
                    blkv = qkT.sub(blk * 512, 512)
                    bn = bank()
                    P.op("pe", lambda e: e.matmul(ps(bn), lhsT=onesb[:, :], rhs=sqs[blk].ap(), start=True, stop=True), reads=[sqs[blk], "onesb"], writes=[PK(bn)])
                    P.op("act", lambda e: e.activation(out=rn.ap(), in_=ps(bn), func=AF.Sqrt, bias=EPS, scale=1.0), reads=[PK(bn)], writes=[rn])
                    P.op("dve", lambda e: e.reciprocal(out=rn.ap(), in_=rn.ap()), reads=[rn], writes=[rn])
                    if blk < 2:
                        qf = qTf.sub(blk * 512, 512)
                        P.op("dve", lambda e: e.scalar_tensor_tensor(out=qf.ap(), in0=blkv.ap(), scalar=float(128 ** -0.5), in1=rn.ap(), op0=ALU.mult, op1=ALU.mult),
                             reads=[blkv, rn], writes=[qf])
                        P.op("pool", lambda e: e.tensor_copy(out=blkv.ap(), in_=qf.ap()), reads=[qf], writes=[blkv])
                    else:
                        P.op("pool", lambda e: e.tensor_tensor(out=blkv.ap(), in0=blkv.ap(), in1=rn.ap(), op=ALU.mult), reads=[blkv, rn], writes=[blkv])

                sqs = [sqb] + sq3
                j = ws_next()
                for blk in range(4):
                    b = bank()
                    proj_fm(j, blk, b)
                    kh = blk % 2
                    conv_block(b, (0 if blk < 2 else 8) + 2 * g + kh, qkT.sub(blk * 512, 512), sqs[blk])
                j = ws_next()
                for hv in range(4):
                    b = bank()
                    proj_fm(j, hv, b)
                    conv_block(b, 16 + 4 * g + hv, vT.sub(hv * 512, 512))
                j = ws_next()
                for hv in range(4):
                    b = bank()
                    proj_fm(j, hv, b)
                    dv = szT.sub(hv * 512, 512)
                    P.op("act", lambda e, b=b, dv=dv: e.activation(out=dv.ap(), in_=ps(b), func=AF.Silu), reads=[PK(b)], writes=[dv])
                def emit_rhsD(pair):
                    for si, sub in enumerate((2 * pair, 2 * pair + 1)):
                        rd = rhsD[si]
                        for hv in range(4):
                            P.op("pool", lambda e, rd=rd, hv=hv, sub=sub: e.tensor_scalar(out=rd.ap()[:, hv * 128:(hv + 1) * 128], in0=cmv(C_SL),
                                                                                         scalar1=col(G_, sub, 4 * g + hv), scalar2=None, op0=ALU.mult),
                                 reads=["cm", sm], writes=[rd])
                emit_rhsD(0)
                for blk in range(4):
                    l2_finish(blk)
                dump(13, qkT.ap(), 2048, [qkT])
                dump(14, qTf.ap(), 1024, [qTf])
                for sub in range(4):
                    b = bank()
                    for hv in range(4):
                        src = vT.sub(hv * 512 + sub * 128, 128)
                        P.op("pe", lambda e, b=b, src=src, hv=hv: e.transpose(psh(b)[:, hv * 128:(hv + 1) * 128], src.ap(), identb[:, :]),
                             reads=[src, "identb"], writes=[PK(b)])
                    dv = vtok.sub(sub * 512, 512)
                    P.op("act", lambda e, b=b, dv=dv: e.activation(out=dv.ap(), in_=psh(b)[:, 0:512], func=AF.Copy), reads=[PK(b)], writes=[dv])
                b = bank()
                for sub in range(4):
                    for kh in range(2):
                        src = qkT.sub((2 + kh) * 512 + sub * 128, 128)
                        P.op("pe", lambda e, b=b, src=src, o=(sub * 2 + kh) * 128: e.transpose(psh(b)[:, o:o + 128], src.ap(), identb[:, :]),
                             reads=[src, "identb"], writes=[PK(b)])
                P.op("act", lambda e, b=b: e.activation(out=ktok.ap(), in_=psh(b)[:, 0:1024], func=AF.Copy), reads=[PK(b)], writes=[ktok])
                dump(15, ktok.ap(), 1024, [ktok])
                dump(16, vtok.ap(), 2048, [vtok])
                dump(17, szT.ap(), 2048, [szT])
                if GDN_STOP == 2:
                    return
                if GDN_STOP == 1.9:
                    return
                dump(15, ktok.ap(), 1024, [ktok])
                dump(16, vtok.ap(), 2048, [vtok])
                dump(17, szT.ap(), 2048, [szT])
                if GDN_STOP == 2:
                    return
                r4 = lambda v: v.r("p (h c) -> p h c", h=4)
                for pair in range(2):
                    subs = (2 * pair, 2 * pair + 1)
                    pv_ = lambda V, si: V.sub(si * 512, 512)
                    if pair == 1:
                        emit_rhsD(1)
                    for si, sub in enumerate(subs):
                        b = bank()
                        for kh in range(2):
                            kT_ = qkT.sub((2 + kh) * 512 + sub * 128, 128)
                            qT_ = qkT.sub(kh * 512 + sub * 128, 128)
                            P.op("pe", lambda e, b=b, kT_=kT_, kh=kh: e.matmul(ps(b)[:, kh * 128:(kh + 1) * 128], lhsT=kT_.ap(), rhs=kT_.ap(), start=True, stop=True),
                                 reads=[kT_], writes=[PK(b)])
                            P.op("pe", lambda e, b=b, kT_=kT_, qT_=qT_, kh=kh: e.matmul(ps(b)[:, (2 + kh) * 128:(3 + kh) * 128], lhsT=qT_.ap(), rhs=kT_.ap(), start=True, stop=True),
                                 reads=[kT_, qT_], writes=[PK(b)])
                        gq = pv_(GQ, si)
                        P.op("dve", lambda e, b=b, gq=gq: e.tensor_tensor(out=gq.ap()[:, 0:256].rearrange("p (h c) -> p h c", h=2),
                                                                           in0=ps(b)[:, 0:256].rearrange("p (h c) -> p h c", h=2),
                                                                           in1=cmv(C_SL).unsqueeze(1).to_broadcast([128, 2, 128]), op=ALU.mult),
                             reads=[PK(b), "cm"], writes=[gq])
                        P.op("dve", lambda e, b=b, gq=gq: e.tensor_tensor(out=gq.ap()[:, 256:512].rearrange("p (h c) -> p h c", h=2),
                                                                           in0=ps(b)[:, 256:512].rearrange("p (h c) -> p h c", h=2),
                                                                           in1=cmv(C_MLI).unsqueeze(1).to_broadcast([128, 2, 128]), op=ALU.mult),
                             reads=[PK(b), "cm"], writes=[gq])
                    if GDN_STOP == 2.1:
                        return
                    for si, sub in enumerate(subs):
                        rd = rhsD[si]
                        b = bank()
                        for hv in range(4):
                            P.op("pe", lambda e, b=b, rd=rd, hv=hv: e.matmul(ps(b)[:, hv * 128:(hv + 1) * 128], lhsT=cmv(C_U), rhs=rd.ap()[:, hv * 128:(hv + 1) * 128],
                                                                             start=True, stop=True),
                                 reads=[rd, "cm"], writes=[PK(b)])
                        ev, pa, at, gq = pv_(X1, si), pv_(Pa, si), pv_(X2, si), pv_(GQ, si)
                        P.op("act", lambda e, b=b, ev=ev: e.activation(out=ev.ap(), in_=ps(b), func=AF.Exp), reads=[PK(b)], writes=[ev])
                        if GDN_STOP == 2.3:
                            dump(21, X1.ap(), 1024, [X1])
                            return
                        for hv in range(4):
                            kh = hv // 2
                            P.op("dve", lambda e, ev=ev, pa=pa, gq=gq, hv=hv, kh=kh, sub=sub: e.scalar_tensor_tensor(
                                out=pa.ap()[:, hv * 128:(hv + 1) * 128], in0=ev.ap()[:, hv * 128:(hv + 1) * 128], scalar=col(NBETA, sub, 4 * g + hv),
                                in1=gq.ap()[:, kh * 128:(kh + 1) * 128], op0=ALU.mult, op1=ALU.mult),
                                reads=[ev, gq, sm], writes=[pa])
                        for kh in range(2):
                            P.op("pool", lambda e, ev=ev, at=at, gq=gq, kh=kh: e.tensor_tensor(
                                out=at.ap()[:, kh * 256:(kh + 1) * 256].rearrange("p (r c) -> p r c", r=2),
                                in0=ev.ap()[:, kh * 256:(kh + 1) * 256].rearrange("p (r c) -> p r c", r=2),
                                in1=gq.ap()[:, 256 + kh * 128:256 + (kh + 1) * 128].unsqueeze(1).to_broadcast([128, 2, 128]), op=ALU.mult),
                                reads=[ev, gq], writes=[at])
                    if GDN_STOP == 2.5:
                        dump(18, Pa.ap(), 1024, [Pa])
                        return
                    for si, sub in enumerate(subs):
                        pa, at, pt, aT, tt = pv_(Pa, si), pv_(X2, si), pv_(PTa, si), pv_(attnT, si), pv_(TTv, si)
                        b = bank()
                        for hv in range(4):
                            P.op("pe", lambda e, b=b, pa=pa, hv=hv: e.transpose(psh(b)[:, hv * 128:(hv + 1) * 128], pa.ap()[:, hv * 128:(hv + 1) * 128], identb[:, :]),
                                 reads=[pa, "identb"], writes=[PK(b)])
                            P.op("pe", lambda e, b=b, at=at, hv=hv: e.transpose(psh(b)[:, 512 + hv * 128:512 + (hv + 1) * 128], at.ap()[:, hv * 128:(hv + 1) * 128], identb[:, :]),
                                 reads=[at, "identb"], writes=[PK(b)])
                        P.op("act", lambda e, b=b, pt=pt: e.activation(out=pt.ap(), in_=psh(b)[:, 0:512], func=AF.Copy), reads=[PK(b)], writes=[pt])
                        P.op("act", lambda e, b=b, aT=aT: e.activation(out=aT.ap(), in_=psh(b)[:, 512:1024], func=AF.Copy), reads=[PK(b)], writes=[aT])
                        po64, pto64, po128, tn = pv_(X2, si), pv_(GQ, si), vnew[si], pv_(X1, si)
                        mk = lambda off: cmv(off).unsqueeze(1).to_broadcast([128, 4, 128])
                        P.op("pool", lambda e, pa=pa, d=po64: e.tensor_tensor(out=r4(d), in0=r4(pa), in1=mk(C_M64), op=ALU.mult), reads=[pa, "cm"], writes=[po64])
                        P.op("pool", lambda e, pt=pt, d=pto64: e.tensor_tensor(out=r4(d), in0=r4(pt), in1=mk(C_M64), op=ALU.mult), reads=[pt, "cm"], writes=[pto64])
                        P.op("pool", lambda e, pa=pa, d=po128: e.tensor_tensor(out=r4(d), in0=r4(pa), in1=mk(C_M128), op=ALU.mult), reads=[pa, "cm"], writes=[po128])
                        P.op("dve", lambda e, pa=pa: e.tensor_tensor(out=r4(pa), in0=r4(pa), in1=mk(C_M32), op=ALU.mult), reads=[pa, "cm"], writes=[pa])
                        P.op("pool", lambda e, pt=pt: e.tensor_tensor(out=r4(pt), in0=r4(pt), in1=mk(C_M32), op=ALU.mult), reads=[pt, "cm"], writes=[pt])
                        P.op("dve", lambda e, pt=pt, tt=tt: e.tensor_tensor(out=r4(tt), in0=r4(pt), in1=mk(C_ID), op=ALU.add), reads=[pt, "cm"], writes=[tt])
                        P.op("pool", lambda e, pa=pa, tn=tn: e.tensor_tensor(out=r4(tn), in0=r4(pa), in1=mk(C_ID), op=ALU.add), reads=[pa, "cm"], writes=[tn])
                        if GDN_STOP == 3:
                            return

                    def mm4(bk, lv, rv):
                        for hv in range(4):
                            hs = slice(hv * 128, (hv + 1) * 128)
                            P.op("pe", lambda e, hs=hs: e.matmul(ps(bk)[:, hs], lhsT=lv.ap()[:, hs], rhs=rv.ap()[:, hs], start=True, stop=True),
                                 reads=[lv, rv], writes=[PK(bk)])

                    def evac(en, bk, dst):
                        if en == "act":
                            P.op("act", lambda e: e.activation(out=dst.ap(), in_=ps(bk), func=AF.Copy), reads=[PK(bk)], writes=[dst])
                        else:
                            P.op("dve", lambda e: e.tensor_copy(out=dst.ap(), in_=ps(bk)), reads=[PK(bk)], writes=[dst])

                    def accum(bk, dst):
                        P.op("dve", lambda e: e.tensor_tensor(out=dst.ap(), in0=dst.ap(), in1=ps(bk), op=ALU.add), reads=[dst, PK(bk)], writes=[dst])

                    for si, sub in enumerate(subs):
                        kb, vb_, kd_ = pv_(kbg, si), pv_(vbv, si), pv_(kdv, si)
                        for hv in range(4):
                            kh, hg = hv // 2, 4 * g + hv
                            hs = slice(hv * 128, (hv + 1) * 128)
                            ksrc = ktok.sub(sub * 256 + kh * 128, 128)
                            vsrc = vtok.sub(sub * 512 + hv * 128, 128)
                            P.op("act", lambda e, kb=kb, hs=hs, ksrc=ksrc, sub=sub, hg=hg: e.activation(out=kb.ap()[:, hs], in_=ksrc.ap(), func=AF.Copy,
                                                                                                     scale=col(BEGE, sub, hg)), reads=[ksrc, sm], writes=[kb])
                            P.op("pool", lambda e, vb_=vb_, hs=hs, vsrc=vsrc, sub=sub, hg=hg: e.tensor_scalar(out=vb_.ap()[:, hs], in0=vsrc.ap(), scalar1=col(BETA, sub, hg),
                                                                                                        scalar2=None, op0=ALU.mult), reads=[vsrc, sm], writes=[vb_])
                            P.op("pool", lambda e, kd_=kd_, hs=hs, ksrc=ksrc, sub=sub, hg=hg: e.tensor_scalar(out=kd_.ap()[:, hs], in0=ksrc.ap(), scalar1=col(EKD, sub, hg),
                                                                                                        scalar2=None, op0=ALU.mult), reads=[ksrc, sm], writes=[kd_])
                    cur, nxt = (Pa, PTa), (Pb, PTb)
                    for lev in range(1, 5):
                        for si, sub in enumerate(subs):
                            p_, pt_, pn, ptn = pv_(cur[0], si), pv_(cur[1], si), pv_(nxt[0], si), pv_(nxt[1], si)
                            b1, b2 = bank(), bank()
                            mm4(b1, pt_, p_)
                            mm4(b2, p_, pt_)
                            evac("act", b1, pn)
                            evac("act", b2, ptn)
                        for si, sub in enumerate(subs):
                            pn, ptn, tt, tn = pv_(nxt[0], si), pv_(nxt[1], si), pv_(TTv, si), pv_(X1, si)
                            b3, b4 = bank(), bank()
                            mm4(b3, pn, tt)
                            mm4(b4, ptn, tn)
                            accum(b3, tt)
                            accum(b4, tn)
                        cur, nxt = nxt, cur
                    for mi, (PO, PTO) in enumerate(((X2, GQ), (None, None))):
                        for si, sub in enumerate(subs):
                            tt, tn, zb, wb = pv_(TTv, si), pv_(X1, si), pv_(Pb, si), pv_(PTb, si)
                            bz = bank()
                            mm4(bz, pv_(PO, si) if PO is not None else vnew[si], tt)
                            evac("act", bz, zb)
                            if PTO is not None:
                                bw = bank()
                                mm4(bw, pv_(PTO, si), tn)
                                evac("act", bw, wb)
                        for si, sub in enumerate(subs):
                            tt, tn, zb, wb = pv_(TTv, si), pv_(X1, si), pv_(Pb, si), pv_(PTb, si)
                            ba = bank()
                            mm4(ba, tn, zb)
                            if PTO is not None:
                                bb_ = bank()
                                mm4(bb_, tt, wb)
                            accum(ba, tt)
                            if PTO is not None:
                                accum(bb_, tn)
                    dump(22, TTv.ap(), 1024, [TTv])
                    if GDN_STOP == 4:
                        return
                    for si, sub in enumerate(subs):
                        kb, tt, nw = pv_(kbg, si), pv_(TTv, si), nwT[si]
                        b = bank()
                        for hv in range(4):
                            hs = slice(hv * 128, (hv + 1) * 128)
                            P.op("pe", lambda e, b=b, hs=hs, kb=kb, tt=tt: e.matmul(ps(b)[:, hs], lhsT=kb.ap()[:, hs], rhs=tt.ap()[:, hs], start=True, stop=True),
                                 reads=[kb, tt], writes=[PK(b)])
                        P.op("act", lambda e, b=b, nw=nw: e.activation(out=nw.ap(), in_=ps(b), func=AF.Copy, scale=-1.0), reads=[PK(b)], writes=[nw])
                    for si, sub in enumerate(subs):
                        vb_, kd_, tt, nw, aT, vn = pv_(vbv, si), pv_(kdv, si), pv_(TTv, si), nwT[si], pv_(attnT, si), vnew[si]
                        SKs = [("gdnS", jl, 4 * g + hv) for hv in range(4)]
                        b = bank()
                        for hv in range(4):
                            hs = slice(hv * 128, (hv + 1) * 128)
                            P.op("pe", lambda e, b=b, hs=hs, tt=tt, vb_=vb_: e.matmul(ps(b)[:, hs], lhsT=tt.ap()[:, hs], rhs=vb_.ap()[:, hs], start=True, stop=False),
                                 reads=[tt, vb_], writes=[PK(b)])
                            P.op("pe", lambda e, b=b, hs=hs, nw=nw, hv=hv: e.matmul(ps(b)[:, hs], lhsT=nw.ap()[:, hs], rhs=gdnS[:, jl, 4 * g + hv, :], start=False, stop=True),
                                 reads=[nw, SKs[hv]], writes=[PK(b)])
                        P.op("act", lambda e, b=b, vn=vn: e.activation(out=vn.ap(), in_=ps(b), func=AF.Copy), reads=[PK(b)], writes=[vn])
                        b1, b2, b3 = bank(), bank(), bank()
                        for hv in range(4):
                            hs = slice(hv * 128, (hv + 1) * 128)
                            kh = hv // 2
                            qf = qTf.sub(kh * 512 + sub * 128, 128)
                            P.op("pe", lambda e, b1=b1, hs=hs, qf=qf, hv=hv: e.matmul(ps(b1)[:, hs], lhsT=qf.ap(), rhs=gdnS[:, jl, 4 * g + hv, :], start=True, stop=True),
                                 reads=[qf, SKs[hv]], writes=[PK(b1)])
                            P.op("pe", lambda e, b2=b2, hs=hs, aT=aT, vn=vn: e.matmul(ps(b2)[:, hs], lhsT=aT.ap()[:, hs], rhs=vn.ap()[:, hs], start=True, stop=True),
                                 reads=[aT, vn], writes=[PK(b2)])
                            P.op("pe", lambda e, b3=b3, hs=hs, kd_=kd_, vn=vn: e.matmul(ps(b3)[:, hs], lhsT=kd_.ap()[:, hs], rhs=vn.ap()[:, hs], start=True, stop=True),
                                 reads=[kd_, vn], writes=[PK(b3)])
                        ov = oraw.sub(sub * 512, 512)
                        for hv in range(4):
                            hs = slice(hv * 128, (hv + 1) * 128)
                            P.op("act", lambda e, b1=b1, hs=hs, sub=sub, hv=hv: e.activation(out=tmpo.ap()[:, hs], in_=ps(b1)[:, hs], func=AF.Copy,
                                                                                       scale=col(EGC, sub, 4 * g + hv)), reads=[PK(b1), sm], writes=[tmpo])
                        P.op("dve", lambda e, b2=b2, ov=ov: e.tensor_tensor(out=ov.ap(), in0=tmpo.ap(), in1=ps(b2), op=ALU.add), reads=[tmpo, PK(b2)], writes=[ov])
                        for hv in range(4):
                            hs = slice(hv * 128, (hv + 1) * 128)
                            P.op("dve", lambda e, b3=b3, hs=hs, sub=sub, hv=hv: e.scalar_tensor_tensor(out=gdnS[:, jl, 4 * g + hv, :], in0=gdnS[:, jl, 4 * g + hv, :],
                                                                                                 scalar=col(EGL, sub, 4 * g + hv), in1=ps(b3)[:, hs], op0=ALU.mult, op1=ALU.add),
                                 reads=[SKs[hv], PK(b3), sm], writes=[SKs[hv]])
                            P.op("act", lambda e, ov=ov, hs=hs, sub=sub, hv=hv: e.activation(out=junk.ap(), in_=ov.ap()[:, hs], func=AF.Square,
                                                                                       accum_out=ssv.ap()[:, sub * 4 + hv:sub * 4 + hv + 1]),
                                 reads=[ov], writes=[junk, ssv])
                    dump(24, gdnS[:, jl, 0:4, :].rearrange("p h c -> p (h c)"), 512, [("gdnS", jl, hh) for hh in range(4)])
                    if GDN_STOP == 5:
                        return
                dump(23, oraw.ap(), 2048, [oraw])
                P.op("act", lambda e: e.activation(out=ssv.ap()[:, 0:16], in_=ssv.ap()[:, 0:16], func=AF.Sqrt, bias=EPS, scale=1.0 / 128), reads=[ssv], writes=[ssv])
                P.op("dve", lambda e: e.reciprocal(out=ssv.ap()[:, 0:16], in_=ssv.ap()[:, 0:16]), reads=[ssv], writes=[ssv])
                on = vT
                P.op("dve", lambda e: e.tensor_tensor(out=on.r("p (a c) -> p a c", a=16), in0=oraw.r("p (a c) -> p a c", a=16),
                                                      in1=ssv.ap()[:, 0:16].unsqueeze(2).to_broadcast([128, 16, 128]), op=ALU.mult),
                     reads=[oraw, ssv], writes=[on])
                for hv in range(4):
                    b = bank()
                    for sub in range(4):
                        src = on.sub(sub * 512 + hv * 128, 128)
                        P.op("pe", lambda e, b=b, src=src, sub=sub: e.transpose(psh(b)[:, sub * 128:(sub + 1) * 128], src.ap(), identb[:, :]),
                             reads=[src, "identb"], writes=[PK(b)])
                    dv = szT.sub(hv * 512, 512)
                    P.op("dve", lambda e, b=b, dv=dv: e.scalar_tensor_tensor(out=dv.ap(), in0=psh(b)[:, 0:512], scalar=prc(O_GNG + jl), in1=dv.ap(),
                                                                             op0=ALU.mult, op1=ALU.mult),
                         reads=[PK(b), dv, "prm"], writes=[dv])
                j = ws_next()
                out_proj(j, szT, 4, 1024)
                dump(25, hT[:, :, :].rearrange("p c t -> p (c t)"), 4096, HT)

            for g in range(4):
                do_group(g)

        xTv = xT.rearrange("(c p) t -> p c t", p=128)
        oTv = outT.rearrange("(c p) t -> p c t", p=128)
        for t in range(n_tiles):
            t0 = t * TT
            P.op("sp", lambda e, t0=t0: e.dma_start(out=hT[:, :, :], in_=xTv[:, :, t0:t0 + TT]), writes=HT, dma="xin")
            P.op("sp", lambda e, t0=t0: e.dma_start(out=posi[:, :], in_=pos[:, t0:t0 + TT].partition_broadcast(128)), writes=["posi"], dma="pin")
            for l in layers:
                if l % 2 == 0:
                    ret_layer(l)
                else:
                    gdn_layer(l)
                ffn(l)
            FF.reset()
            BB.reset()
            ob = FF.alloc(4096)
            if final_norm:
                rmsnorm(O_NFIN, dst_f32=ob)
            else:
                for c in range(8):
                    P.op("act", lambda e, c=c: e.activation(out=ob.sub(c * 512, 512).ap(), in_=hT[:, c, :], func=AF.Copy),
                         reads=[("hT", c)], writes=[ob.sub(c * 512, 512)])
            P.op("sp", lambda e, t0=t0: e.dma_start(out=oTv[:, :, t0:t0 + TT], in_=ob.r("p (c t) -> p c t", c=8)), reads=[ob], writes=[("out", t % 2)], dma=f"xout{t % 2}")
        P.op("sp", None, reads=[("out", 0), ("out", 1)] + [("dbg", i) for i in dcount])
        P.emit(nc, es)
    return nc


def _layer_slots(l):
    base = 0
    for i in range(l):
        base += SLOTS_RET if i % 2 == 0 else SLOTS_GDN
    n = SLOTS_RET if l % 2 == 0 else SLOTS_GDN
    return list(range(base, base + n))


N_CORES = 8
GDN_STOP = 99


def kernel(x, positions, norm_mix, norm_ffn, norm_final, ret_w_in, ret_gn_gain, ret_w_out,
           gdn_w_in, gdn_conv, gdn_a_log, gdn_dt_bias, gdn_norm_gain, gdn_w_out, ffn_w_in, ffn_w_out):
    inp = dict(norm_mix=norm_mix, norm_ffn=norm_ffn, norm_final=norm_final, ret_w_in=ret_w_in, ret_gn_gain=ret_gn_gain,
               ret_w_out=ret_w_out, gdn_w_in=gdn_w_in, gdn_conv=gdn_conv, gdn_a_log=gdn_a_log, gdn_dt_bias=gdn_dt_bias,
               gdn_norm_gain=gdn_norm_gain, gdn_w_out=gdn_w_out, ffn_w_in=ffn_w_in, ffn_w_out=ffn_w_out)
    inp = {k: np.asarray(v, np.float32) for k, v in inp.items()}
    x = np.asarray(x, np.float32)
    positions = np.asarray(positions, np.int32)
    wall = pack_weights(inp)
    prm = pack_params(inp)
    cmat = const_mats()
    nc = build_program()
    in_maps = []
    for c in range(N_CORES):
        b = c % BATCH
        in_maps.append({"xT": np.ascontiguousarray(x[b].T), "pos": np.ascontiguousarray(positions[b][None, :]),
                        "prm": prm, "cmat": cmat, "wall": wall})
    res = run_bass_kernel_spmd(nc, in_maps, core_ids=list(range(N_CORES)))
    out = np.stack([np.ascontiguousarray(res.results[b]["outT"].T) for b in range(BATCH)], 0)
    return out.astype(np.float32)
```

```python
import numpy as np
from contextlib import ExitStack
import concourse.bass as bass
import concourse.mybir as mybir
from concourse.bass_utils import run_bass_kernel_spmd

F32 = mybir.dt.float32
BF16 = mybir.dt.bfloat16
I32 = mybir.dt.int32
AF = mybir.ActivationFunctionType
ALU = mybir.AluOpType

D = 1024
SEQ = 4096
BATCH = 4
DEPTH = 4
TT = 512
NSUB = 4
FFH = 2816
NHB = 22
EPS = 1e-6
GAM = [1.0 - 2.0 ** (-5 - h) for h in range(4)]

O_NM, O_NF, O_NFIN, O_RGN, O_GNG, O_CW, O_ALOG, O_DTB, O_INVF, O_GK = 0, 32, 64, 72, 104, 106, 362, 394, 426, 427
NPRM = 431
C_ID, C_U, C_SL, C_MLI, C_ONES, C_DECT, C_GQ, C_M32, C_M64, C_M128 = 0, 128, 256, 384, 512, 640, 1152, 1664, 1792, 1920
NCM = 2048

SLOTS_RET = 16 + 19
SLOTS_GDN = 17 + 19
NSLOT = 2 * SLOTS_RET + 2 * SLOTS_GDN

SEM_LIM = 30000
ENGS = ("pe", "act", "dve", "pool", "sp")


def _slot_in(W, cols):
    n = len(cols)
    s = np.zeros((128, 4096), np.float32)
    s[:, :8 * n] = W[:, cols].reshape(8, 128, n).transpose(1, 0, 2).reshape(128, 8 * n)
    return s


def _slot_out_mixer(W, r0):
    return W[r0:r0 + 512].reshape(4, 128, 1024).transpose(1, 0, 2).reshape(128, 4096)


def _slot_out_ffn(W, d):
    s = np.zeros((128, 4096), np.float32)
    s[:, :NHB * 128] = W[:, d * 128:(d + 1) * 128].reshape(NHB, 128, 128).transpose(1, 0, 2).reshape(128, NHB * 128)
    return s


def _ffn_slots(w_in, w_out):
    out = []
    for s in range(11):
        cols = []
        for hb in (2 * s, 2 * s + 1):
            cols += list(range(hb * 128, hb * 128 + 128))
            cols += list(range(FFH + hb * 128, FFH + hb * 128 + 128))
        out.append(_slot_in(w_in, np.array(cols)))
    for d in range(8):
        out.append(_slot_out_ffn(w_out, d))
    return out


def pack_weights(inp):
    slots = []
    ar = np.arange
    for l in range(DEPTH):
        j = l // 2
        if l % 2 == 0:
            wi, wo = inp["ret_w_in"][j], inp["ret_w_out"][j]
            for h in range(4):
                slots.append(_slot_in(wi, np.concatenate([h * 256 + ar(256), 1024 + h * 256 + ar(256)])))
                slots.append(_slot_in(wi, 2048 + h * 512 + ar(512)))
                slots.append(_slot_in(wi, 4096 + h * 512 + ar(512)))
                slots.append(_slot_out_mixer(wo, h * 512))
        else:
            wi, wo = inp["gdn_w_in"][j], inp["gdn_w_out"][j]
            slots.append(_slot_in(wi, 6144 + ar(32)))
            for g in range(4):
                slots.append(_slot_in(wi, np.concatenate([g * 256 + ar(256), 1024 + g * 256 + ar(256)])))
                slots.append(_slot_in(wi, 2048 + g * 512 + ar(512)))
                slots.append(_slot_in(wi, 4096 + g * 512 + ar(512)))
                slots.append(_slot_out_mixer(wo, g * 512))
        slots += _ffn_slots(inp["ffn_w_in"][l], inp["ffn_w_out"][l])
    assert len(slots) == NSLOT
    return np.ascontiguousarray(np.stack(slots, 0))


def pack_params(inp):
    p = np.zeros((128, NPRM), np.float32)
    fm = lambda v: np.asarray(v, np.float32).reshape(-1, 128).T
    for l in range(4):
        p[:, O_NM + 8 * l:O_NM + 8 * l + 8] = fm(inp["norm_mix"][l])
        p[:, O_NF + 8 * l:O_NF + 8 * l + 8] = fm(inp["norm_ffn"][l])
    p[:, O_NFIN:O_NFIN + 8] = fm(inp["norm_final"])
    for j in range(2):
        p[:, O_RGN + 16 * j:O_RGN + 16 * j + 16] = fm(inp["ret_gn_gain"][j])
        p[:, O_GNG + j] = np.asarray(inp["gdn_norm_gain"][j], np.float32)
        cw = np.asarray(inp["gdn_conv"][j], np.float32)
        p[:, O_CW + 128 * j:O_CW + 128 * j + 128] = cw.reshape(4, 32, 128).transpose(2, 1, 0).reshape(128, 128)
        p[:, O_ALOG + 16 * j:O_ALOG + 16 * j + 16] = np.asarray(inp["gdn_a_log"][j], np.float32)[None, :]
        p[:, O_DTB + 16 * j:O_DTB + 16 * j + 16] = np.asarray(inp["gdn_dt_bias"][j], np.float32)[None, :]
    i = np.arange(128, dtype=np.float64)
    p[:, O_INVF] = (10000.0 ** (-i / 128.0) / (2 * np.pi)).astype(np.float32)
    for h in range(4):
        p[:, O_GK + h] = (GAM[h] ** (127.0 - i)).astype(np.float32)
    return p


def const_mats():
    c = np.zeros((128, NCM), np.float32)
    r = np.arange(128)[:, None].astype(np.float64)
    q = np.arange(128)[None, :].astype(np.float64)
    c[:, C_ID:C_ID + 128] = (r == q)
    c[:, C_U:C_U + 128] = (r <= q)
    c[:, C_SL:C_SL + 128] = (r > q)
    c[:, C_MLI:C_MLI + 128] = (r >= q)
    c[:, C_ONES:C_ONES + 128] = 1.0
    c[:, C_M32:C_M32 + 128] = (r // 32 == q // 32)
    c[:, C_M64:C_M64 + 128] = (r // 64 == q // 64) & (r // 32 != q // 32)
    c[:, C_M128:C_M128 + 128] = (r // 64 != q // 64)
    for h in range(4):
        c[:, C_DECT + 128 * h:C_DECT + 128 * h + 128] = np.where(q >= r, GAM[h] ** np.maximum(q - r, 0), 0.0)
        c[:, C_GQ + 128 * h:C_GQ + 128 * h + 128] = np.broadcast_to(GAM[h] ** (q + 1.0), (128, 128))
    return c


class View:
    small = False

    def __init__(self, arena, lo, n):
        self.arena, self.lo, self.n = arena, lo, n

    def ap(self):
        return self.arena.t[:, self.lo:self.lo + self.n]

    def r(self, pat, **kw):
        return self.ap().rearrange(pat, **kw)

    def sub(self, off, n):
        assert off + n <= self.n
        return View(self.arena, self.lo + off, n)

    def keys(self):
        g = self.arena.G
        return [(self.arena.name, i) for i in range(self.lo // g, (self.lo + self.n - 1) // g + 1)]


class Arena:
    def __init__(self, name, t, G):
        self.name, self.t, self.G, self.ptr = name, t, G, 0

    def alloc(self, n, small=False):
        lo = (self.ptr + self.G - 1) // self.G * self.G
        self.ptr = lo + n
        assert self.ptr <= self.t.shape[1], (self.name, self.ptr, self.t.shape)
        v = View(self, lo, n)
        v.small = small
        return v

    def reset(self, to=0):
        self.ptr = to


def _has_small(items):
    for it in items:
        if isinstance(it, View):
            if it.small:
                return True
        elif isinstance(it, list) and _has_small(it):
            return True
    return False


def _keys(items):
    out = []
    for it in items:
        if isinstance(it, View):
            out.extend(it.keys())
        elif isinstance(it, list) or (isinstance(it, tuple) and not isinstance(it[0], str)):
            out.extend(_keys(it))
        else:
            out.append(it)
    return out


class Op:
    __slots__ = ("eng", "fn", "waits", "signal", "dma_key", "dma_val", "idx")


class Prog:
    def __init__(self):
        self.ops = {e: [] for e in ENGS}
        self.last_w, self.readers = {}, {}
        self.known = {e: {} for e in ENGS}
        self.dma_cnt = {}

    def op(self, eng, fn, reads=(), writes=(), dma=None):
        o = Op()
        o.eng, o.fn, o.signal, o.dma_key, o.dma_val = eng, fn, False, dma, 0
        lst = self.ops[eng]
        lst.append(o)
        o.idx = len(lst)
        rk, wk = _keys(reads), _keys(writes)
        need = {}
        known = self.known[eng]
        ss = _has_small(reads)

        def want(tok, raw=False):
            sk, val = tok
            if (sk == eng and eng == "pe") or known.get(sk, 0) >= val:
                return
            if need.get(sk, 0) < val:
                need[sk] = val

        for k in rk:
            t = self.last_w.get(k)
            if t is not None:
                want(t, True)
        for k in wk:
            t = self.last_w.get(k)
            if t is not None:
                want(t)
            for t in self.readers.get(k, ()):
                want(t)
        if dma is not None and self.dma_cnt.get(dma, 0) > 0:
            want(("dma:" + dma, self.dma_cnt[dma]))
        for sk, val in need.items():
            known[sk] = val
        o.waits = list(need.items())
        if dma is not None:
            c = self.dma_cnt.get(dma, 0) + 16
            self.dma_cnt[dma] = c
            o.dma_val = c
            tok = ("dma:" + dma, c)
        else:
            tok = (eng, o.idx)
        for k in rk:
            lst = self.readers.setdefault(k, [])
            lst[:] = [t for t in lst if t[0] != tok[0]]
            lst.append(tok)
        for k in wk:
            self.last_w[k] = tok
            self.readers[k] = []
        return o

    def emit(self, nc, es):
        for e in ENGS:
            for o in self.ops[e]:
                for sk, val in o.waits:
                    if not sk.startswith("dma:"):
                        self.ops[sk][val - 1].signal = True
        sig = {}
        for e in ENGS:
            n, l = 0, []
            for o in self.ops[e]:
                n += 1 if o.signal else 0
                l.append(n)
            sig[e] = l
        sems = {}
        for e in ENGS:
            tot = sig[e][-1] if sig[e] else 0
            for ep in range((tot + SEM_LIM - 1) // SEM_LIM):
                sems[(e, ep)] = es.enter_context(nc.semaphore(f"s_{e}_{ep}"))
        for k in self.dma_cnt:
            assert self.dma_cnt[k] < 60000, (k, self.dma_cnt[k])
            sems["dma:" + k] = es.enter_context(nc.semaphore("d_" + k))
        block = es.enter_context(nc.Block())

        def run(en):
            def body(eng):
                for o in self.ops[en]:
                    for sk, val in o.waits:
                        if sk.startswith("dma:"):
                            eng.wait_ge(sems[sk], val)
                        else:
                            s = sig[sk][val - 1]
                            eng.wait_ge(sems[(sk, (s - 1) // SEM_LIM)], (s - 1) % SEM_LIM + 1)
                    if o.fn is None:
                        continue
                    ins = o.fn(eng)
                    if o.dma_key is not None:
                        ins.then_inc(sems["dma:" + o.dma_key], 16)
                    elif o.signal:
                        s = sig[en][o.idx - 1]
                        ins.then_inc(sems[(en, (s - 1) // SEM_LIM)], 1)
            return body

        block.tensor(run("pe"))
        block.scalar(run("act"))
        block.vector(run("dve"))
        block.gpsimd(run("pool"))
        block.sync(run("sp"))


def build_program(n_tiles=SEQ // TT, layers=(0, 1, 2, 3), final_norm=True, debug=False, nslot=NSLOT):
    nc = bass.Bass("TRN2", target_bir_lowering=False)
    NSLOT_ = nslot
    xT = nc.dram_tensor("xT", [D, SEQ], F32, kind="ExternalInput").ap()
    pos = nc.dram_tensor("pos", [1, SEQ], I32, kind="ExternalInput").ap()
    prm_d = nc.dram_tensor("prm", [128, NPRM], F32, kind="ExternalInput").ap()
    cm_d = nc.dram_tensor("cmat", [128, NCM], F32, kind="ExternalInput").ap()
    wall = nc.dram_tensor("wall", [NSLOT_, 128, 4096], F32, kind="ExternalInput").ap()
    dbg = nc.dram_tensor("dbg", [32, 128, 4096], F32, kind="ExternalOutput").ap() if debug else None
    dbgb = nc.dram_tensor("dbgb", [32, 128, 4096], BF16, kind="ExternalOutput").ap() if debug else None
    outT = nc.dram_tensor("outT", [D, SEQ], F32, kind="ExternalOutput").ap()
    wsc = nc.dram_tensor("wsc", [NSLOT_, 128, 4096], BF16, kind="Internal").ap()

    P = Prog()
    with ExitStack() as es:
        sb = lambda n, s, d: es.enter_context(nc.sbuf_tensor(n, s, d))
        hT = sb("hT", [128, 8, TT], F32)
        hnT = sb("hnT", [128, 8, TT], BF16)
        wsl = sb("wsl", [128, 3, 4096], BF16)
        retS = sb("retS", [128, 2, 4, 2, 512], F32)
        gdnS = sb("gdnS", [128, 2, 16, 128], F32)
        hist = sb("hist", [128, 2, 32, 3], F32)
        prm = sb("prm_sb", [128, NPRM], F32)
        cm = sb("cm_sb", [128, NCM], F32)
        identb = sb("identb", [128, 128], BF16)
        onesb = sb("onesb", [128, 128], BF16)
        nexpA = sb("nexpA", [128, 2, 16], F32)
        posi = sb("posi", [128, TT], I32)
        yi = sb("yi", [128, TT], I32)
        BBt = sb("BBt", [128, 12 * 2048], BF16)
        FFt = sb("FFt", [128, 23 * 512], F32)
        BB = Arena("BB", BBt, 512)
        FF = Arena("FF", FFt, 512)
        psb = [es.enter_context(nc.psum_tensor(f"ps{i}", [128, 512], F32)) for i in range(8)]
        bank_ctr = [0]

        def bank():
            b = bank_ctr[0] % 8
            bank_ctr[0] += 1
            return b

        ps = lambda b: psb[b][:, :]
        psh = lambda b: psb[b][:, :].bitcast(BF16)
        PK = lambda b: ("ps", b)
        cmv = lambda off, n=128: cm[:, off:off + n]
        prc = lambda off, n=1: prm[:, off:off + n]

        P.op("sp", lambda e: e.dma_start(out=prm[:, :], in_=prm_d[:, :]), writes=["prm"], dma="ldprm")
        P.op("sp", lambda e: e.dma_start(out=cm[:, :], in_=cm_d[:, :]), writes=["cm"], dma="ldcm")
        for s in range(NSLOT_):
            P.op("pool", lambda e, s=s: e.dma_start(out=wsc[s], in_=wall[s]), writes=[("wsc", s)], dma=f"cv{s % (2 if s < 6 else 8)}")
        P.op("dve", lambda e: e.tensor_copy(out=identb[:, :], in_=cmv(C_ID)), reads=["cm"], writes=["identb"])
        P.op("dve", lambda e: e.tensor_copy(out=onesb[:, :], in_=cmv(C_ONES)), reads=["cm"], writes=["onesb"])
        P.op("dve", lambda e: e.memset(retS[:, :, :, :, :], 0.0), writes=["retS"])
        P.op("dve", lambda e: e.memset(gdnS[:, :, :, :], 0.0), writes=["gdnS"])
        P.op("dve", lambda e: e.memset(hist[:, :, :, :], 0.0), writes=["hist"])
        P.op("act", lambda e: e.activation(out=nexpA[:, :, :], in_=prm[:, O_ALOG:O_ALOG + 32].rearrange("p (j h) -> p j h", j=2), func=AF.Exp),
             reads=["prm"], writes=["nexpA"])
        P.op("dve", lambda e: e.tensor_scalar(out=nexpA[:, :, :], in0=nexpA[:, :, :], scalar1=-1.0, scalar2=None, op0=ALU.mult),
             reads=["nexpA"], writes=["nexpA"])

        dcount = {}

        def dump(idx, ap, n, reads):
            if not debug or dcount.get(idx):
                return
            dcount[idx] = 1
            dst = dbgb if ap.dtype == BF16 else dbg
            P.op("sp", lambda e: e.dma_start(out=dst[idx][:, 0:n], in_=ap), reads=reads, writes=[("dbg", idx)], dma=f"dbg{idx}")

        stream = [s for _ in range(n_tiles) for l in layers for s in _layer_slots(l)]
        st = {"n": 0, "issued": 0}

        def ws_next():
            n = st["n"]
            st["n"] += 1
            while st["issued"] < min(n + 3, len(stream)):
                m = st["issued"]
                s, j = stream[m], m % 3
                P.op("sp", lambda e, s=s, j=j: e.dma_start(out=wsl[:, j, :], in_=wsc[s]),
                     reads=[("wsc", s)], writes=[("wsl", j)], dma=f"w{j}")
                st["issued"] += 1
            return n % 3

        WK = lambda j: ("wsl", j)
        HT = [("hT", c) for c in range(8)]
        HN = [("hnT", c) for c in range(8)]

        def rmsnorm(goff, dst_f32=None):
            for q4 in range(4):
                cs = slice(q4 * 2, q4 * 2 + 2)
                if q4 % 2 == 0:
                    P.op("act", lambda e, cs=cs: e.activation(out=hnT[:, cs, :], in_=hT[:, cs, :], func=AF.Square),
                         reads=HT[cs], writes=HN[cs])
                else:
                    P.op("dve", lambda e, cs=cs: e.tensor_tensor(out=hnT[:, cs, :], in0=hT[:, cs, :], in1=hT[:, cs, :], op=ALU.mult),
                         reads=HT[cs], writes=HN[cs])
            b = bank()
            for c in range(8):
                P.op("pe", lambda e, c=c, b=b: e.matmul(ps(b), lhsT=onesb[:, :], rhs=hnT[:, c, :], start=(c == 0), stop=(c == 7)),
                     reads=[("hnT", c), "onesb"], writes=[PK(b)])
            rs = FF.alloc(512)
            P.op("act", lambda e, b=b, rs=rs: e.activation(out=rs.ap(), in_=ps(b), func=AF.Sqrt, bias=EPS, scale=1.0 / D),
                 reads=[PK(b)], writes=[rs])
            P.op("dve", lambda e, rs=rs: e.reciprocal(out=rs.ap(), in_=rs.ap()), reads=[rs], writes=[rs])
            for c in range(8):
                en = "dve"
                if dst_f32 is None:
                    P.op(en, lambda e, c=c, rs=rs: e.scalar_tensor_tensor(out=hnT[:, c, :], in0=hT[:, c, :], scalar=prc(goff + c), in1=rs.ap(),
                                                                          op0=ALU.mult, op1=ALU.mult),
                         reads=[("hT", c), rs, "prm"], writes=[("hnT", c)])
                else:
                    dv = dst_f32.sub(c * 512, 512)
                    P.op(en, lambda e, c=c, rs=rs, dv=dv: e.scalar_tensor_tensor(out=dv.ap(), in0=hT[:, c, :], scalar=prc(goff + c), in1=rs.ap(),
                                                                                op0=ALU.mult, op1=ALU.mult),
                         reads=[("hT", c), rs, "prm"], writes=[dv])

        def proj_fm(j, blk, b):
            for kc in range(8):
                P.op("pe", lambda e, kc=kc: e.matmul(ps(b), lhsT=wsl[:, j, kc * 512 + blk * 128:kc * 512 + blk * 128 + 128], rhs=hnT[:, kc, :],
                                                    start=(kc == 0), stop=(kc == 7)),
                     reads=[WK(j), ("hnT", kc)], writes=[PK(b)])

        def out_proj(j, yv, nck, stride):
            for d in range(8):
                b = bank()
                for c in range(nck):
                    off = c * stride + d * 128
                    P.op("pe", lambda e, c=c, off=off, b=b: e.matmul(ps(b), lhsT=wsl[:, j, off:off + 128], rhs=yv.sub(c * 512, 512).ap(),
                                                                     start=(c == 0), stop=(c == nck - 1)),
                         reads=[WK(j), yv.sub(c * 512, 512)], writes=[PK(b)])
                P.op("dve", lambda e, d=d, b=b: e.tensor_tensor(out=hT[:, d, :], in0=hT[:, d, :], in1=ps(b), op=ALU.add),
                     reads=[("hT", d), PK(b)], writes=[("hT", d)])

        def ffn(l):
            FF.reset()
            BB.reset()
            rmsnorm(O_NF + 8 * l)
            act = BB.alloc(NHB * 512)
            sgs = [FF.alloc(512) for _ in range(4)]
            for s in range(11):
                j = ws_next()
                for w in range(2):
                    hb = 2 * s + w
                    bg, bu = bank(), bank()
                    proj_fm(j, 2 * w, bg)
                    proj_fm(j, 2 * w + 1, bu)
                    sg = sgs[(2 * s + w) % 4]
                    av = act.sub(hb * 512, 512)
                    P.op("act", lambda e, bg=bg, sg=sg: e.activation(out=sg.ap(), in_=ps(bg), func=AF.Silu), reads=[PK(bg)], writes=[sg])
                    P.op("dve", lambda e, bu=bu, sg=sg, av=av: e.tensor_tensor(out=av.ap(), in0=sg.ap(), in1=ps(bu), op=ALU.mult),
                         reads=[sg, PK(bu)], writes=[av])
            for d in range(8):
                j = ws_next()
                b = bank()
                for c in range(NHB):
                    P.op("pe", lambda e, c=c, b=b, j=j: e.matmul(ps(b), lhsT=wsl[:, j, c * 128:c * 128 + 128], rhs=act.sub(c * 512, 512).ap(),
                                                                 start=(c == 0), stop=(c == NHB - 1)),
                         reads=[WK(j), act.sub(c * 512, 512)], writes=[PK(b)])
                P.op("dve", lambda e, d=d, b=b: e.tensor_tensor(out=hT[:, d, :], in0=hT[:, d, :], in1=ps(b), op=ALU.add),
                     reads=[("hT", d), PK(b)], writes=[("hT", d)])

        def ret_layer(l):
            jl = l // 2
            FF.reset()
            BB.reset()
            rmsnorm(O_NM + 8 * l)
            cosT, sinT, cosq, sinq = [FF.alloc(512) for _ in range(4)]
            ff_base = FF.ptr
            yv, fr = FF.alloc(512), FF.alloc(512)
            dump(0, hnT[:, :, :].rearrange("p c t -> p (c t)"), 4096, HN)
            P.op("dve", lambda e: e.tensor_copy(out=yv.ap(), in_=posi[:, :]), reads=["posi"], writes=[yv])
            P.op("dve", lambda e: e.tensor_scalar(out=yv.ap(), in0=yv.ap(), scalar1=prc(O_INVF), scalar2=None, op0=ALU.mult),
                 reads=[yv, "prm"], writes=[yv])
            P.op("dve", lambda e: e.tensor_copy(out=yi[:, :], in_=yv.ap()), reads=[yv], writes=["yi"])
            P.op("dve", lambda e: e.tensor_copy(out=fr.ap(), in_=yi[:, :]), reads=["yi"], writes=[fr])
            P.op("dve", lambda e: e.tensor_tensor(out=yv.ap(), in0=yv.ap(), in1=fr.ap(), op=ALU.subtract), reads=[yv, fr], writes=[yv])
            TWO_PI = 6.28318
            P.op("act", lambda e: e.activation(out=sinT.ap(), in_=yv.ap(), func=AF.Sin, scale=TWO_PI), reads=[yv], writes=[sinT])
            P.op("dve", lambda e: e.tensor_scalar(out=yv.ap(), in0=yv.ap(), scalar1=0.25, scalar2=None, op0=ALU.add), reads=[yv], writes=[yv])
            P.op("dve", lambda e: e.tensor_scalar(out=fr.ap(), in0=yv.ap(), scalar1=0.5, scalar2=None, op0=ALU.is_gt), reads=[yv], writes=[fr])
            P.op("dve", lambda e: e.tensor_tensor(out=yv.ap(), in0=yv.ap(), in1=fr.ap(), op=ALU.subtract), reads=[yv, fr], writes=[yv])
            P.op("act", lambda e: e.activation(out=cosT.ap(), in_=yv.ap(), func=AF.Sin, scale=TWO_PI), reads=[yv], writes=[cosT])
            P.op("act", lambda e: e.mul(out=cosq.ap(), in_=cosT.ap(), mul=1.0 / 16), reads=[cosT], writes=[cosq])
            P.op("act", lambda e: e.mul(out=sinq.ap(), in_=sinT.ap(), mul=1.0 / 16), reads=[sinT], writes=[sinq])
            bb_base = BB.ptr
            dump(1, cosT.ap(), 512, [cosT])
            dump(2, sinT.ap(), 512, [sinT])
            def do_head(h):
                FF.reset(ff_base)
                BB.reset(bb_base)
                qk = BB.alloc(2048)
                kd = BB.alloc(1024)
                vsb = BB.alloc(2048)
                sg = BB.alloc(2048)
                on = BB.alloc(2048)
                sT = BB.alloc(512)
                oraw = FF.alloc(2048)
                stt = FF.alloc(512, small=True)
                junk = FF.alloc(512)
                qd = [FF.alloc(256) for _ in range(2)]
                rsh = [FF.alloc(512) for _ in range(4)]
                rt = [[FF.alloc(512), FF.alloc(512)] + rsh for _ in range(2)]
                j = ws_next()
                for pair in range(2):
                    b1, b2 = bank(), bank()
                    proj_fm(j, 2 * pair, b1)
                    proj_fm(j, 2 * pair + 1, b2)
                    s1, s2, ta, tb, tc, td = rt[pair]
                    ct, sn = (cosq, sinq) if pair == 0 else (cosT, sinT)
                    o1, o2 = qk.sub(2 * pair * 512, 512), qk.sub((2 * pair + 1) * 512, 512)
                    P.op("act", lambda e, b1=b1, s1=s1: e.activation(out=s1.ap(), in_=ps(b1), func=AF.Copy), reads=[PK(b1)], writes=[s1])
                    P.op("act", lambda e, b2=b2, s2=s2: e.activation(out=s2.ap(), in_=ps(b2), func=AF.Copy), reads=[PK(b2)], writes=[s2])
                    P.op("dve", lambda e, s1=s1, ct=ct, ta=ta: e.tensor_tensor(out=ta.ap(), in0=s1.ap(), in1=ct.ap(), op=ALU.mult), reads=[s1, ct], writes=[ta])
                    P.op("dve", lambda e, s2=s2, sn=sn, tb=tb: e.tensor_tensor(out=tb.ap(), in0=s2.ap(), in1=sn.ap(), op=ALU.mult), reads=[s2, sn], writes=[tb])
                    P.op("pool", lambda e, s2=s2, ct=ct, tc=tc: e.tensor_tensor(out=tc.ap(), in0=s2.ap(), in1=ct.ap(), op=ALU.mult), reads=[s2, ct], writes=[tc])
                    P.op("pool", lambda e, s1=s1, sn=sn, td=td: e.tensor_tensor(out=td.ap(), in0=s1.ap(), in1=sn.ap(), op=ALU.mult), reads=[s1, sn], writes=[td])
                    P.op("dve", lambda e, ta=ta, tb=tb, o1=o1: e.tensor_tensor(out=o1.ap(), in0=ta.ap(), in1=tb.ap(), op=ALU.subtract), reads=[ta, tb], writes=[o1])
                    P.op("pool", lambda e, tc=tc, td=td, o2=o2: e.tensor_tensor(out=o2.ap(), in0=tc.ap(), in1=td.ap(), op=ALU.add), reads=[tc, td], writes=[o2])
                j = ws_next()
                for sub in range(4):
                    b = bank()
                    for kc in range(8):
                        P.op("pe", lambda e, kc=kc, b=b, sub=sub, j=j: e.matmul(ps(b), lhsT=hnT[:, kc, sub * 128:(sub + 1) * 128],
                                                                               rhs=wsl[:, j, kc * 512:(kc + 1) * 512], start=(kc == 0), stop=(kc == 7)),
                             reads=[WK(j), ("hnT", kc)], writes=[PK(b)])
                    dv = vsb.sub(sub * 512, 512)
                    if sub % 2 == 0:
                        P.op("act", lambda e, b=b, dv=dv: e.activation(out=dv.ap(), in_=ps(b), func=AF.Copy), reads=[PK(b)], writes=[dv])
                    else:
                        P.op("dve", lambda e, b=b, dv=dv: e.tensor_copy(out=dv.ap(), in_=ps(b)), reads=[PK(b)], writes=[dv])
                j = ws_next()
                for blk in range(4):
                    b = bank()
                    proj_fm(j, blk, b)
                    dv = sg.sub(blk * 512, 512)
                    P.op("act", lambda e, b=b, dv=dv: e.activation(out=dv.ap(), in_=ps(b), func=AF.Silu), reads=[PK(b)], writes=[dv])
                b = bank()
                for sub in range(4):
                    for dch in range(2):
                        src = qk.sub((2 + dch) * 512 + sub * 128, 128)
                        P.op("pe", lambda e, b=b, src=src, o=(sub * 2 + dch) * 128: e.transpose(psh(b)[:, o:o + 128], src.ap(), identb[:, :]),
                             reads=[src, "identb"], writes=[PK(b)])
                P.op("act", lambda e, b=b, h=h: e.activation(out=kd.ap(), in_=psh(b)[:, 0:1024], func=AF.Copy, scale=prc(O_GK + h)),
                     reads=[PK(b), "prm"], writes=[kd])
                dump(3, qk.ap(), 2048, [qk])
                dump(4, kd.ap(), 1024, [kd])
                dump(5, vsb.ap(), 2048, [vsb])
                dump(6, sg.ap(), 2048, [sg])
                SK = ("retS", jl, h)
                for sub in range(4):
                    cs = slice(sub * 128, sub * 128 + 128)
                    q1, q2 = qk.sub(sub * 128, 128), qk.sub(512 + sub * 128, 128)
                    k1, k2 = qk.sub(1024 + sub * 128, 128), qk.sub(1536 + sub * 128, 128)
                    b = bank()
                    P.op("pe", lambda e, b=b, k1=k1, q1=q1: e.matmul(ps(b)[:, 0:128], lhsT=k1.ap(), rhs=q1.ap(), start=True, stop=False),
                         reads=[k1, q1], writes=[PK(b)])
                    P.op("pe", lambda e, b=b, k2=k2, q2=q2: e.matmul(ps(b)[:, 0:128], lhsT=k2.ap(), rhs=q2.ap(), start=False, stop=True),
                         reads=[k2, q2], writes=[PK(b)])
                    sTv = sT.sub(sub * 128, 128)
                    P.op("dve", lambda e, b=b, sTv=sTv, h=h: e.tensor_tensor(out=sTv.ap(), in0=ps(b)[:, 0:128], in1=cmv(C_DECT + 128 * h), op=ALU.mult),
                         reads=[PK(b), "cm"], writes=[sTv])
                    qdv = qd[sub % 2]
                    P.op("pool", lambda e, qdv=qdv, sub=sub, h=h: e.tensor_tensor(
                        out=qdv.r("p (c i) -> p c i", c=2),
                        in0=qk.sub(0, 1024).r("p (c t) -> p c t", c=2)[:, :, sub * 128:(sub + 1) * 128],
                        in1=cmv(C_GQ + 128 * h).unsqueeze(1).to_broadcast([128, 2, 128]), op=ALU.mult),
                        reads=[qk.sub(sub * 128, 128), qk.sub(512 + sub * 128, 128), "cm"], writes=[qdv])
                    vv = vsb.sub(sub * 512, 512)
                    b2 = bank()
                    P.op("pe", lambda e, b2=b2, sTv=sTv, vv=vv: e.matmul(ps(b2), lhsT=sTv.ap(), rhs=vv.ap(), start=True, stop=False),
                         reads=[sTv, vv], writes=[PK(b2)])
                    for dch in range(2):
                        P.op("pe", lambda e, b2=b2, qdv=qdv, dch=dch, h=h: e.matmul(ps(b2), lhsT=qdv.ap()[:, dch * 128:(dch + 1) * 128],
                                                                                   rhs=retS[:, jl, h, dch, :], start=False, stop=(dch == 1)),
                             reads=[qdv, SK], writes=[PK(b2)])
                    for dch in range(2):
                        b3 = bank()
                        kdv = kd.sub(sub * 256 + dch * 128, 128)
                        P.op("pe", lambda e, b3=b3, kdv=kdv, vv=vv: e.matmul(ps(b3), lhsT=kdv.ap(), rhs=vv.ap(), start=True, stop=True),
                             reads=[kdv, vv], writes=[PK(b3)])
                        P.op("dve", lambda e, b3=b3, dch=dch, h=h: e.scalar_tensor_tensor(out=retS[:, jl, h, dch, :], in0=retS[:, jl, h, dch, :],
                                                                                         scalar=float(GAM[h] ** 128), in1=ps(b3), op0=ALU.mult, op1=ALU.add),
                             reads=[SK, PK(b3)], writes=[SK])
                    ov = oraw.sub(sub * 512, 512)
                    P.op("act", lambda e, b2=b2, ov=ov, sub=sub: e.activation(out=ov.ap(), in_=ps(b2), func=AF.Copy, accum_out=stt.ap()[:, sub:sub + 1]),
                         reads=[PK(b2)], writes=[ov, stt])
                    P.op("act", lambda e, ov=ov, sub=sub: e.activation(out=junk.ap(), in_=ov.ap(), func=AF.Square, accum_out=stt.ap()[:, 4 + sub:5 + sub]),
                         reads=[ov], writes=[junk, stt])
                S_ = stt.ap()
                P.op("dve", lambda e: e.tensor_scalar(out=S_[:, 8:12], in0=S_[:, 0:4], scalar1=1.0 / 512, scalar2=None, op0=ALU.mult), reads=[stt], writes=[stt])
                P.op("dve", lambda e: e.tensor_tensor(out=S_[:, 20:24], in0=S_[:, 8:12], in1=S_[:, 8:12], op=ALU.mult), reads=[stt], writes=[stt])
                P.op("dve", lambda e: e.scalar_tensor_tensor(out=S_[:, 12:16], in0=S_[:, 4:8], scalar=1.0 / 512, in1=S_[:, 20:24], op0=ALU.mult, op1=ALU.subtract),
                     reads=[stt], writes=[stt])
                P.op("dve", lambda e: e.tensor_scalar(out=S_[:, 12:16], in0=S_[:, 12:16], scalar1=0.0, scalar2=None, op0=ALU.max), reads=[stt], writes=[stt])
                P.op("act", lambda e: e.activation(out=S_[:, 12:16], in_=S_[:, 12:16], func=AF.Sqrt, bias=EPS, scale=1.0), reads=[stt], writes=[stt])
                P.op("dve", lambda e: e.reciprocal(out=S_[:, 12:16], in_=S_[:, 12:16]), reads=[stt], writes=[stt])
                P.op("dve", lambda e: e.scalar_tensor_tensor(out=S_[:, 16:20], in0=S_[:, 8:12], scalar=-1.0, in1=S_[:, 12:16], op0=ALU.mult, op1=ALU.mult),
                     reads=[stt], writes=[stt])
                for sub in range(4):
                    ov, nv = oraw.sub(sub * 512, 512), on.sub(sub * 512, 512)
                    en = "dve" if sub % 2 == 0 else "pool"
                    P.op(en, lambda e, ov=ov, nv=nv, sub=sub: e.tensor_scalar(out=nv.ap(), in0=ov.ap(), scalar1=S_[:, 12 + sub:13 + sub],
                                                                             scalar2=S_[:, 16 + sub:17 + sub], op0=ALU.mult, op1=ALU.add),
                         reads=[ov, stt], writes=[nv])
                dump(7, oraw.ap(), 2048, [oraw])
                dump(8, stt.ap(), 512, [stt])
                dump(9, on.ap(), 2048, [on])
                for vb in range(4):
                    b = bank()
                    for sub in range(4):
                        src = on.sub(sub * 512 + vb * 128, 128)
                        P.op("pe", lambda e, b=b, src=src, sub=sub: e.transpose(psh(b)[:, sub * 128:(sub + 1) * 128], src.ap(), identb[:, :]),
                             reads=[src, "identb"], writes=[PK(b)])
                    dv = sg.sub(vb * 512, 512)
                    P.op("dve", lambda e, b=b, dv=dv, c=O_RGN + 16 * jl + 4 * h + vb: e.scalar_tensor_tensor(
                        out=dv.ap(), in0=psh(b)[:, 0:512], scalar=prc(c), in1=dv.ap(), op0=ALU.mult, op1=ALU.mult),
                        reads=[PK(b), dv, "prm"], writes=[dv])
                dump(10, sg.ap(), 2048, [sg])
                j = ws_next()
                out_proj(j, sg, 4, 1024)
                dump(11, hT[:, :, :].rearrange("p c t -> p (c t)"), 4096, HT)

            for h in range(4):
                do_head(h)

        def gdn_layer(l):
            jl = l // 2
            FF.reset()
            BB.reset()
            sm = FF.alloc(1024, small=True)
            ff_base = FF.ptr
            rmsnorm(O_NM + 8 * l)
            fld = lambda i: sm.ap()[:, i * 64:(i + 1) * 64].rearrange("p (s h) -> p s h", s=4)
            if debug:
                P.op("pool", lambda e: e.memset(sm.ap(), 0.0), writes=[sm])
            BL, A_, BETA, NBETA, X_, AX, E_, L_, G_, GC, GL, EGC, EGL, EKD, BEGE, TMP = range(16)
            j = ws_next()
            b = bank()
            for sub in range(4):
                for kc in range(8):
                    P.op("pe", lambda e, b=b, sub=sub, kc=kc, j=j: e.matmul(ps(b)[:, sub * 32:(sub + 1) * 32], lhsT=hnT[:, kc, sub * 128:(sub + 1) * 128],
                                                                           rhs=wsl[:, j, kc * 32:(kc + 1) * 32], start=(kc == 0), stop=(kc == 7)),
                         reads=[WK(j), ("hnT", kc)], writes=[PK(b)])
            pv = lambda b, lo: ps(b)[:, 0:128].rearrange("p (s c) -> p s c", c=32)[:, :, lo:lo + 16]
            bc16 = lambda ap2: ap2.unsqueeze(1).to_broadcast([128, 4, 16])
            P.op("act", lambda e, b=b: e.activation(out=fld(BETA), in_=pv(b, 0), func=AF.Sigmoid), reads=[PK(b)], writes=[sm])
            P.op("dve", lambda e, b=b: e.tensor_tensor(out=fld(X_), in0=pv(b, 16), in1=bc16(prm[:, O_DTB + 16 * jl:O_DTB + 16 * jl + 16]), op=ALU.add),
                 reads=[PK(b), "prm"], writes=[sm])
            P.op("act", lambda e: e.activation(out=fld(AX), in_=fld(X_), func=AF.Abs), reads=[sm], writes=[sm])
            P.op("act", lambda e: e.activation(out=fld(E_), in_=fld(AX), func=AF.Exp, scale=-1.0), reads=[sm], writes=[sm])
            P.op("act", lambda e: e.activation(out=fld(L_), in_=fld(E_), func=AF.Ln, bias=1.0, scale=1.0), reads=[sm], writes=[sm])
            P.op("dve", lambda e: e.tensor_scalar(out=fld(TMP), in0=fld(X_), scalar1=0.0, scalar2=None, op0=ALU.max), reads=[sm], writes=[sm])
            P.op("dve", lambda e: e.tensor_tensor(out=fld(TMP), in0=fld(TMP), in1=fld(L_), op=ALU.add), reads=[sm], writes=[sm])
            P.op("dve", lambda e: e.tensor_tensor(out=fld(G_), in0=fld(TMP), in1=bc16(nexpA[:, jl, :]), op=ALU.mult), reads=[sm, "nexpA"], writes=[sm])
            b2 = bank()
            for sub in range(4):
                P.op("pe", lambda e, b2=b2, sub=sub: e.matmul(ps(b2)[:, sub * 32:sub * 32 + 16], lhsT=cmv(C_U), rhs=fld(G_)[:, sub, :], start=True, stop=True),
                     reads=[sm, "cm"], writes=[PK(b2)])
                P.op("pe", lambda e, b2=b2, sub=sub: e.matmul(ps(b2)[:, sub * 32 + 16:sub * 32 + 32], lhsT=cmv(C_ONES), rhs=fld(G_)[:, sub, :], start=True, stop=True),
                     reads=[sm, "cm"], writes=[PK(b2)])
            P.op("act", lambda e, b2=b2: e.activation(out=fld(GC), in_=pv(b2, 0), func=AF.Copy), reads=[PK(b2)], writes=[sm])
            P.op("act", lambda e, b2=b2: e.activation(out=fld(GL), in_=pv(b2, 16), func=AF.Copy), reads=[PK(b2)], writes=[sm])
            P.op("act", lambda e: e.activation(out=fld(EGC), in_=fld(GC), func=AF.Exp), reads=[sm], writes=[sm])
            P.op("act", lambda e: e.activation(out=fld(EGL), in_=fld(GL), func=AF.Exp), reads=[sm], writes=[sm])
            P.op("dve", lambda e: e.tensor_tensor(out=fld(TMP), in0=fld(GL), in1=fld(GC), op=ALU.subtract), reads=[sm], writes=[sm])
            P.op("act", lambda e: e.activation(out=fld(EKD), in_=fld(TMP), func=AF.Exp), reads=[sm], writes=[sm])
            P.op("dve", lambda e: e.tensor_tensor(out=fld(BEGE), in0=fld(BETA), in1=fld(EGC), op=ALU.mult), reads=[sm], writes=[sm])
            P.op("dve", lambda e: e.tensor_scalar(out=fld(NBETA), in0=fld(BETA), scalar1=-1.0, scalar2=None, op0=ALU.mult), reads=[sm], writes=[sm])
            col = lambda f, sub, hg: sm.ap()[:, f * 64 + sub * 16 + hg:f * 64 + sub * 16 + hg + 1]
            dump(12, sm.ap(), 1024, [sm])
            if GDN_STOP == 0:
                return

            bb_base = BB.ptr

            def do_group(g):
                FF.reset(ff_base)
                BB.reset(bb_base)
                qkT = BB.alloc(2048)
                vT = BB.alloc(2048)
                vtok = BB.alloc(2048)
                ktok = BB.alloc(1024)
                szT = BB.alloc(2048)
                sqb = BB.alloc(512)
                sq3 = [BB.alloc(512) for _ in range(3)]
                GQ = BB.alloc(1024)
                X1, X2, Pa, Pb, PTa, PTb, TTv, attnT, kbg, vbv, kdv = [BB.alloc(1024) for _ in range(11)]
                vnew = [BB.alloc(512) for _ in range(2)]
                xb2 = FF.alloc(1040)
                xbs = [xb2.sub(0, 520), xb2.sub(520, 520)]
                accs = [FF.alloc(512) for _ in range(2)]
                css = [FF.alloc(512) for _ in range(2)]
                rhsD = [FF.alloc(512) for _ in range(2)]
                nwT = [FF.alloc(512) for _ in range(2)]
                qTf = FF.alloc(1024)
                oraw = FF.alloc(2048)
                tmpo = FF.alloc(512)
                sj = FF.alloc(512, small=True)
                ssv, junk = sj.sub(0, 64), sj.sub(64, 128)
                ssv.small = True
                cnt = [0]

                def conv_block(b, bg, dst, sq=None):
                    i = cnt[0] % 2
                    cnt[0] += 1
                    xb, acc = xbs[i], accs[i]
                    HK = ("hist", jl, bg)
                    P.op("act", lambda e: e.activation(out=xb.ap()[:, 3:515], in_=ps(b), func=AF.Copy), reads=[PK(b)], writes=[xb])
                    P.op("pool", lambda e: e.tensor_copy(out=xb.ap()[:, 0:3], in_=hist[:, jl, bg, :]), reads=[HK], writes=[xb])
                    P.op("pool", lambda e: e.tensor_copy(out=hist[:, jl, bg, :], in_=xb.ap()[:, 512:515]), reads=[xb], writes=[HK])
                    cw = lambda k: prc(O_CW + 128 * jl + bg * 4 + k)
                    P.op("dve", lambda e: e.tensor_scalar(out=acc.ap(), in0=xb.ap()[:, 3:515], scalar1=cw(3), scalar2=None, op0=ALU.mult),
                         reads=[xb, "prm"], writes=[acc])
                    for k in (2, 1, 0):
                        P.op("dve", lambda e, k=k: e.scalar_tensor_tensor(out=acc.ap(), in0=xb.ap()[:, k:k + 512], scalar=cw(k), in1=acc.ap(),
                                                                          op0=ALU.mult, op1=ALU.add),
                             reads=[xb, acc, "prm"], writes=[acc])
                    P.op("act", lambda e: e.activation(out=dst.ap(), in_=acc.ap(), func=AF.Silu), reads=[acc], writes=[dst])
                    if sq is not None:
                        P.op("act", lambda e: e.activation(out=sq.ap(), in_=dst.ap(), func=AF.Square), reads=[dst], writes=[sq])

                def l2_finish(blk):
                    rn = css[blk % 2]
                    blkv = qkT.sub(blk * 512, 512)
                    bn = bank()
                    P.op("pe", lambda e: e.matmul(ps(bn), lhsT=onesb[:, :], rhs=sqs[blk].ap(), start=True, stop=True), reads=[sqs[blk], "onesb"], writes=[PK(bn)])
                    P.op("act", lambda e: e.activation(out=rn.ap(), in_=ps(bn), func=AF.Sqrt, bias=EPS, scale=1.0), reads=[PK(bn)], writes=[rn])
                    P.op("dve", lambda e: e.reciprocal(out=rn.ap(), in_=rn.ap()), reads=[rn], writes=[rn])
                    if blk < 2:
                        qf = qTf.sub(blk * 512, 512)
                        P.op("dve", lambda e: e.scalar_tensor_tensor(out=qf.ap(), in0=blkv.ap(), scalar=float(128 ** -0.5), in1=rn.ap(), op0=ALU.mult, op1=ALU.mult),
                             reads=[blkv, rn], writes=[qf])
                        P.op("pool", lambda e: e.tensor_copy(out=blkv.ap(), in_=qf.ap()), reads=[qf], writes=[blkv])
                    else:
                        P.op("pool", lambda e: e.tensor_tensor(out=blkv.ap(), in0=blkv.ap(), in1=rn.ap(), op=ALU.mult), reads=[blkv, rn], writes=[blkv])

                sqs = [sqb] + sq3
                j = ws_next()
                for blk in range(4):
                    b = bank()
                    proj_fm(j, blk, b)
                    kh = blk % 2
                    conv_block(b, (0 if blk < 2 else 8) + 2 * g + kh, qkT.sub(blk * 512, 512), sqs[blk])
                j = ws_next()
                for hv in range(4):
                    b = bank()
                    proj_fm(j, hv, b)
                    conv_block(b, 16 + 4 * g + hv, vT.sub(hv * 512, 512))
                j = ws_next()
                for hv in range(4):
                    b = bank()
                    proj_fm(j, hv, b)
                    dv = szT.sub(hv * 512, 512)
                    P.op("act", lambda e, b=b, dv=dv: e.activation(out=dv.ap(), in_=ps(b), func=AF.Silu), reads=[PK(b)], writes=[dv])
                def emit_rhsD(pair):
                    for si, sub in enumerate((2 * pair, 2 * pair + 1)):
                        rd = rhsD[si]
                        for hv in range(4):
                            P.op("pool", lambda e, rd=rd, hv=hv, sub=sub: e.tensor_scalar(out=rd.ap()[:, hv * 128:(hv + 1) * 128], in0=cmv(C_SL),
                                                                                         scalar1=col(G_, sub, 4 * g + hv), scalar2=None, op0=ALU.mult),
                                 reads=["cm", sm], writes=[rd])
                emit_rhsD(0)
                for blk in range(4):
                    l2_finish(blk)
                dump(13, qkT.ap(), 2048, [qkT])
                dump(14, qTf.ap(), 1024, [qTf])
                for sub in range(4):
                    b = bank()
                    for hv in range(4):
                        src = vT.sub(hv * 512 + sub * 128, 128)
                        P.op("pe", lambda e, b=b, src=src, hv=hv: e.transpose(psh(b)[:, hv * 128:(hv + 1) * 128], src.ap(), identb[:, :]),
                             reads=[src, "identb"], writes=[PK(b)])
                    dv = vtok.sub(sub * 512, 512)
                    P.op("act", lambda e, b=b, dv=dv: e.activation(out=dv.ap(), in_=psh(b)[:, 0:512], func=AF.Copy), reads=[PK(b)], writes=[dv])
                b = bank()
                for sub in range(4):
                    for kh in range(2):
                        src = qkT.sub((2 + kh) * 512 + sub * 128, 128)
                        P.op("pe", lambda e, b=b, src=src, o=(sub * 2 + kh) * 128: e.transpose(psh(b)[:, o:o + 128], src.ap(), identb[:, :]),
                             reads=[src, "identb"], writes=[PK(b)])
                P.op("act", lambda e, b=b: e.activation(out=ktok.ap(), in_=psh(b)[:, 0:1024], func=AF.Copy), reads=[PK(b)], writes=[ktok])
                dump(15, ktok.ap(), 1024, [ktok])
                dump(16, vtok.ap(), 2048, [vtok])
                dump(17, szT.ap(), 2048, [szT])
                if GDN_STOP == 2:
                    return
                if GDN_STOP == 1.9:
                    return
                dump(15, ktok.ap(), 1024, [ktok])
                dump(16, vtok.ap(), 2048, [vtok])
                dump(17, szT.ap(), 2048, [szT])
                if GDN_STOP == 2:
                    return
                r4 = lambda v: v.r("p (h c) -> p h c", h=4)
                for pair in range(2):
                    subs = (2 * pair, 2 * pair + 1)
                    pv_ = lambda V, si: V.sub(si * 512, 512)
                    if pair == 1:
                        emit_rhsD(1)
                    for si, sub in enumerate(subs):
                        b = bank()
                        for kh in range(2):
                            kT_ = qkT.sub((2 + kh) * 512 + sub * 128, 128)
                            qT_ = qkT.sub(kh * 512 + sub * 128, 128)
                            P.op("pe", lambda e, b=b, kT_=kT_, kh=kh: e.matmul(ps(b)[:, kh * 128:(kh + 1) * 128], lhsT=kT_.ap(), rhs=kT_.ap(), start=True, stop=True),
                                 reads=[kT_], writes=[PK(b)])
                            P.op("pe", lambda e, b=b, kT_=kT_, qT_=qT_, kh=kh: e.matmul(ps(b)[:, (2 + kh) * 128:(3 + kh) * 128], lhsT=qT_.ap(), rhs=kT_.ap(), start=True, stop=True),
                                 reads=[kT_, qT_], writes=[PK(b)])
                        gq = pv_(GQ, si)
                        P.op("dve", lambda e, b=b, gq=gq: e.tensor_tensor(out=gq.ap()[:, 0:256].rearrange("p (h c) -> p h c", h=2),
                                                                           in0=ps(b)[:, 0:256].rearrange("p (h c) -> p h c", h=2),
                                                                           in1=cmv(C_SL).unsqueeze(1).to_broadcast([128, 2, 128]), op=ALU.mult),
                             reads=[PK(b), "cm"], writes=[gq])
                        P.op("dve", lambda e, b=b, gq=gq: e.tensor_tensor(out=gq.ap()[:, 256:512].rearrange("p (h c) -> p h c", h=2),
                                                                           in0=ps(b)[:, 256:512].rearrange("p (h c) -> p h c", h=2),
                                                                           in1=cmv(C_MLI).unsqueeze(1).to_broadcast([128, 2, 128]), op=ALU.mult),
                             reads=[PK(b), "cm"], writes=[gq])
                    if GDN_STOP == 2.1:
                        return
                    for si, sub in enumerate(subs):
                        rd = rhsD[si]
                        b = bank()
                        for hv in range(4):
                            P.op("pe", lambda e, b=b, rd=rd, hv=hv: e.matmul(ps(b)[:, hv * 128:(hv + 1) * 128], lhsT=cmv(C_U), rhs=rd.ap()[:, hv * 128:(hv + 1) * 128],
                                                                             start=True, stop=True),
                                 reads=[rd, "cm"], writes=[PK(b)])
                        ev, pa, at, gq = pv_(X1, si), pv_(Pa, si), pv_(X2, si), pv_(GQ, si)
                        P.op("act", lambda e, b=b, ev=ev: e.activation(out=ev.ap(), in_=ps(b), func=AF.Exp), reads=[PK(b)], writes=[ev])
                        if GDN_STOP == 2.3:
                            dump(21, X1.ap(), 1024, [X1])
                            return
                        for hv in range(4):
                            kh = hv // 2
                            P.op("dve", lambda e, ev=ev, pa=pa, gq=gq, hv=hv, kh=kh, sub=sub: e.scalar_tensor_tensor(
                                out=pa.ap()[:, hv * 128:(hv + 1) * 128], in0=ev.ap()[:, hv * 128:(hv + 1) * 128], scalar=col(NBETA, sub, 4 * g + hv),
                                in1=gq.ap()[:, kh * 128:(kh + 1) * 128], op0=ALU.mult, op1=ALU.mult),
                                reads=[ev, gq, sm], writes=[pa])
                        for kh in range(2):
                            P.op("pool", lambda e, ev=ev, at=at, gq=gq, kh=kh: e.tensor_tensor(
                                out=at.ap()[:, kh * 256:(kh + 1) * 256].rearrange("p (r c) -> p r c", r=2),
                                in0=ev.ap()[:, kh * 256:(kh + 1) * 256].rearrange("p (r c) -> p r c", r=2),
                                in1=gq.ap()[:, 256 + kh * 128:256 + (kh + 1) * 128].unsqueeze(1).to_broadcast([128, 2, 128]), op=ALU.mult),
                                reads=[ev, gq], writes=[at])
                    if GDN_STOP == 2.5:
                        dump(18, Pa.ap(), 1024, [Pa])
                        return
                    for si, sub in enumerate(subs):
                        pa, at, pt, aT, tt = pv_(Pa, si), pv_(X2, si), pv_(PTa, si), pv_(attnT, si), pv_(TTv, si)
                        b = bank()
                        for hv in range(4):
                            P.op("pe", lambda e, b=b, pa=pa, hv=hv: e.transpose(psh(b)[:, hv * 128:(hv + 1) * 128], pa.ap()[:, hv * 128:(hv + 1) * 128], identb[:, :]),
                                 reads=[pa, "identb"], writes=[PK(b)])
                            P.op("pe", lambda e, b=b, at=at, hv=hv: e.transpose(psh(b)[:, 512 + hv * 128:512 + (hv + 1) * 128], at.ap()[:, hv * 128:(hv + 1) * 128], identb[:, :]),
                                 reads=[at, "identb"], writes=[PK(b)])
                        P.op("act", lambda e, b=b, pt=pt: e.activation(out=pt.ap(), in_=psh(b)[:, 0:512], func=AF.Copy), reads=[PK(b)], writes=[pt])
                        P.op("act", lambda e, b=b, aT=aT: e.activation(out=aT.ap(), in_=psh(b)[:, 512:1024], func=AF.Copy), reads=[PK(b)], writes=[aT])
                        po64, pto64, po128, tn = pv_(X2, si), pv_(GQ, si), vnew[si], pv_(X1, si)
                        mk = lambda off: cmv(off).unsqueeze(1).to_broadcast([128, 4, 128])
                        P.op("pool", lambda e, pa=pa, d=po64: e.tensor_tensor(out=r4(d), in0=r4(pa), in1=mk(C_M64), op=ALU.mult), reads=[pa, "cm"], writes=[po64])
                        P.op("pool", lambda e, pt=pt, d=pto64: e.tensor_tensor(out=r4(d), in0=r4(pt), in1=mk(C_M64), op=ALU.mult), reads=[pt, "cm"], writes=[pto64])
                        P.op("pool", lambda e, pa=pa, d=po128: e.tensor_tensor(out=r4(d), in0=r4(pa), in1=mk(C_M128), op=ALU.mult), reads=[pa, "cm"], writes=[po128])
                        P.op("dve", lambda e, pa=pa: e.tensor_tensor(out=r4(pa), in0=r4(pa), in1=mk(C_M32), op=ALU.mult), reads=[pa, "cm"], writes=[pa])
                        P.op("pool", lambda e, pt=pt: e.tensor_tensor(out=r4(pt), in0=r4(pt), in1=mk(C_M32), op=ALU.mult), reads=[pt, "cm"], writes=[pt])
                        P.op("dve", lambda e, pt=pt, tt=tt: e.tensor_tensor(out=r4(tt), in0=r4(pt), in1=mk(C_ID), op=ALU.add), reads=[pt, "cm"], writes=[tt])
                        P.op("pool", lambda e, pa=pa, tn=tn: e.tensor_tensor(out=r4(tn), in0=r4(pa), in1=mk(C_ID), op=ALU.add), reads=[pa, "cm"], writes=[tn])
                        if GDN_STOP == 3:
                            return

                    def mm4(bk, lv, rv):
                        for hv in range(4):
                            hs = slice(hv * 128, (hv + 1) * 128)
                            P.op("pe", lambda e, hs=hs: e.matmul(ps(bk)[:, hs], lhsT=lv.ap()[:, hs], rhs=rv.ap()[:, hs], start=True, stop=True),
                                 reads=[lv, rv], writes=[PK(bk)])

                    def evac(en, bk, dst):
                        if en == "act":
                            P.op("act", lambda e: e.activation(out=dst.ap(), in_=ps(bk), func=AF.Copy), reads=[PK(bk)], writes=[dst])
                        else:
                            P.op("dve", lambda e: e.tensor_copy(out=dst.ap(), in_=ps(bk)), reads=[PK(bk)], writes=[dst])

                    def accum(bk, dst):
                        P.op("dve", lambda e: e.tensor_tensor(out=dst.ap(), in0=dst.ap(), in1=ps(bk), op=ALU.add), reads=[dst, PK(bk)], writes=[dst])

                    for si, sub in enumerate(subs):
                        kb, vb_, kd_ = pv_(kbg, si), pv_(vbv, si), pv_(kdv, si)
                        for hv in range(4):
                            kh, hg = hv // 2, 4 * g + hv
                            hs = slice(hv * 128, (hv + 1) * 128)
                            ksrc = ktok.sub(sub * 256 + kh * 128, 128)
                            vsrc = vtok.sub(sub * 512 + hv * 128, 128)
                            P.op("act", lambda e, kb=kb, hs=hs, ksrc=ksrc, sub=sub, hg=hg: e.activation(out=kb.ap()[:, hs], in_=ksrc.ap(), func=AF.Copy,
                                                                                                     scale=col(BEGE, sub, hg)), reads=[ksrc, sm], writes=[kb])
                            P.op("pool", lambda e, vb_=vb_, hs=hs, vsrc=vsrc, sub=sub, hg=hg: e.tensor_scalar(out=vb_.ap()[:, hs], in0=vsrc.ap(), scalar1=col(BETA, sub, hg),
                                                                                                        scalar2=None, op0=ALU.mult), reads=[vsrc, sm], writes=[vb_])
                            P.op("pool", lambda e, kd_=kd_, hs=hs, ksrc=ksrc, sub=sub, hg=hg: e.tensor_scalar(out=kd_.ap()[:, hs], in0=ksrc.ap(), scalar1=col(EKD, sub, hg),
                                                                                                        scalar2=None, op0=ALU.mult), reads=[ksrc, sm], writes=[kd_])
                    cur, nxt = (Pa, PTa), (Pb, PTb)
                    for lev in range(1, 5):
                        for si, sub in enumerate(subs):
                            p_, pt_, pn, ptn = pv_(cur[0], si), pv_(cur[1], si), pv_(nxt[0], si), pv_(nxt[1], si)
                            b1, b2 = bank(), bank()
                            mm4(b1, pt_, p_)
                            mm4(b2, p_, pt_)
                            evac("act", b1, pn)
                            evac("act", b2, ptn)
                        for si, sub in enumerate(subs):
                            pn, ptn, tt, tn = pv_(nxt[0], si), pv_(nxt[1], si), pv_(TTv, si), pv_(X1, si)
                            b3, b4 = bank(), bank()
                            mm4(b3, pn, tt)
                            mm4(b4, ptn, tn)
                            accum(b3, tt)
                            accum(b4, tn)
                        cur, nxt = nxt, cur
                    for mi, (PO, PTO) in enumerate(((X2, GQ), (None, None))):
                        for si, sub in enumerate(subs):
                            tt, tn, zb, wb = pv_(TTv, si), pv_(X1, si), pv_(Pb, si), pv_(PTb, si)
                            bz = bank()
                            mm4(bz, pv_(PO, si) if PO is not None else vnew[si], tt)
                            evac("act", bz, zb)
                            if PTO is not None:
                                bw = bank()
                                mm4(bw, pv_(PTO, si), tn)
                                evac("act", bw, wb)
                        for si, sub in enumerate(subs):
                            tt, tn, zb, wb = pv_(TTv, si), pv_(X1, si), pv_(Pb, si), pv_(PTb, si)
                            ba = bank()
                            mm4(ba, tn, zb)
                            if PTO is not None:
                                bb_ = bank()
                                mm4(bb_, tt, wb)
                            accum(ba, tt)
                            if PTO is not None:
                                accum(bb_, tn)
                    dump(22, TTv.ap(), 1024, [TTv])
                    if GDN_STOP == 4:
                        return
                    for si, sub in enumerate(subs):
                        kb, tt, nw = pv_(kbg, si), pv_(TTv, si), nwT[si]
                        b = bank()
                        for hv in range(4):
                            hs = slice(hv * 128, (hv + 1) * 128)
                            P.op("pe", lambda e, b=b, hs=hs, kb=kb, tt=tt: e.matmul(ps(b)[:, hs], lhsT=kb.ap()[:, hs], rhs=tt.ap()[:, hs], start=True, stop=True),
                                 reads=[kb, tt], writes=[PK(b)])
                        P.op("act", lambda e, b=b, nw=nw: e.activation(out=nw.ap(), in_=ps(b), func=AF.Copy, scale=-1.0), reads=[PK(b)], writes=[nw])
                    for si, sub in enumerate(subs):
                        vb_, kd_, tt, nw, aT, vn = pv_(vbv, si), pv_(kdv, si), pv_(TTv, si), nwT[si], pv_(attnT, si), vnew[si]
                        SKs = [("gdnS", jl, 4 * g + hv) for hv in range(4)]
                        b = bank()
                        for hv in range(4):
                            hs = slice(hv * 128, (hv + 1) * 128)
                            P.op("pe", lambda e, b=b, hs=hs, tt=tt, vb_=vb_: e.matmul(ps(b)[:, hs], lhsT=tt.ap()[:, hs], rhs=vb_.ap()[:, hs], start=True, stop=False),
                                 reads=[tt, vb_], writes=[PK(b)])
                            P.op("pe", lambda e, b=b, hs=hs, nw=nw, hv=hv: e.matmul(ps(b)[:, hs], lhsT=nw.ap()[:, hs], rhs=gdnS[:, jl, 4 * g + hv, :], start=False, stop=True),
                                 reads=[nw, SKs[hv]], writes=[PK(b)])
                        P.op("act", lambda e, b=b, vn=vn: e.activation(out=vn.ap(), in_=ps(b), func=AF.Copy), reads=[PK(b)], writes=[vn])
                        b1, b2, b3 = bank(), bank(), bank()
                        for hv in range(4):
                            hs = slice(hv * 128, (hv + 1) * 128)
                            kh = hv // 2
                            qf = qTf.sub(kh * 512 + sub * 128, 128)
                            P.op("pe", lambda e, b1=b1, hs=hs, qf=qf, hv=hv: e.matmul(ps(b1)[:, hs], lhsT=qf.ap(), rhs=gdnS[:, jl, 4 * g + hv, :], start=True, stop=True),
                                 reads=[qf, SKs[hv]], writes=[PK(b1)])
                        for hv in range(4):
                            hs = slice(hv * 128, (hv + 1) * 128)
                            P.op("pe", lambda e, b3=b3, hs=hs, kd_=kd_, vn=vn: e.matmul(ps(b3)[:, hs], lhsT=kd_.ap()[:, hs], rhs=vn.ap()[:, hs], start=True, stop=True),
                                 reads=[kd_, vn], writes=[PK(b3)])
                        for hv in range(4):
                            hs = slice(hv * 128, (hv + 1) * 128)
                            P.op("pe", lambda e, b2=b2, hs=hs, aT=aT, vn=vn: e.matmul(ps(b2)[:, hs], lhsT=aT.ap()[:, hs], rhs=vn.ap()[:, hs], start=True, stop=True),
                                 reads=[aT, vn], writes=[PK(b2)])
                        ov = oraw.sub(sub * 512, 512)
                        for hv in range(4):
                            hs = slice(hv * 128, (hv + 1) * 128)
                            P.op("dve", lambda e, b3=b3, hs=hs, sub=sub, hv=hv: e.scalar_tensor_tensor(out=gdnS[:, jl, 4 * g + hv, :], in0=gdnS[:, jl, 4 * g + hv, :],
                                                                                                 scalar=col(EGL, sub, 4 * g + hv), in1=ps(b3)[:, hs], op0=ALU.mult, op1=ALU.add),
                                 reads=[SKs[hv], PK(b3), sm], writes=[SKs[hv]])
                        for hv in range(4):
                            hs = slice(hv * 128, (hv + 1) * 128)
                            P.op("act", lambda e, b1=b1, hs=hs, sub=sub, hv=hv: e.activation(out=tmpo.ap()[:, hs], in_=ps(b1)[:, hs], func=AF.Copy,
                                                                                       scale=col(EGC, sub, 4 * g + hv)), reads=[PK(b1), sm], writes=[tmpo])
                        P.op("dve", lambda e, b2=b2, ov=ov: e.tensor_tensor(out=ov.ap(), in0=tmpo.ap(), in1=ps(b2), op=ALU.add), reads=[tmpo, PK(b2)], writes=[ov])
                        for hv in range(4):
                            hs = slice(hv * 128, (hv + 1) * 128)
                            P.op("act", lambda e, ov=ov, hs=hs, sub=sub, hv=hv: e.activation(out=junk.ap(), in_=ov.ap()[:, hs], func=AF.Square,
                                                                                       accum_out=ssv.ap()[:, sub * 4 + hv:sub * 4 + hv + 1]),
                                 reads=[ov], writes=[junk, ssv])
                    dump(24, gdnS[:, jl, 0:4, :].rearrange("p h c -> p (h c)"), 512, [("gdnS", jl, hh) for hh in range(4)])
                    if GDN_STOP == 5:
                        return
                dump(23, oraw.ap(), 2048, [oraw])
                P.op("act", lambda e: e.activation(out=ssv.ap()[:, 0:16], in_=ssv.ap()[:, 0:16], func=AF.Sqrt, bias=EPS, scale=1.0 / 128), reads=[ssv], writes=[ssv])
                P.op("dve", lambda e: e.reciprocal(out=ssv.ap()[:, 0:16], in_=ssv.ap()[:, 0:16]), reads=[ssv], writes=[ssv])
                on = vT
                P.op("dve", lambda e: e.tensor_tensor(out=on.r("p (a c) -> p a c", a=16), in0=oraw.r("p (a c) -> p a c", a=16),
                                                      in1=ssv.ap()[:, 0:16].unsqueeze(2).to_broadcast([128, 16, 128]), op=ALU.mult),
                     reads=[oraw, ssv], writes=[on])
                for hv in range(4):
                    b = bank()
                    for sub in range(4):
                        src = on.sub(sub * 512 + hv * 128, 128)
                        P.op("pe", lambda e, b=b, src=src, sub=sub: e.transpose(psh(b)[:, sub * 128:(sub + 1) * 128], src.ap(), identb[:, :]),
                             reads=[src, "identb"], writes=[PK(b)])
                    dv = szT.sub(hv * 512, 512)
                    P.op("dve", lambda e, b=b, dv=dv: e.scalar_tensor_tensor(out=dv.ap(), in0=psh(b)[:, 0:512], scalar=prc(O_GNG + jl), in1=dv.ap(),
                                                                             op0=ALU.mult, op1=ALU.mult),
                         reads=[PK(b), dv, "prm"], writes=[dv])
                j = ws_next()
                out_proj(j, szT, 4, 1024)
                dump(25, hT[:, :, :].rearrange("p c t -> p (c t)"), 4096, HT)

            for g in range(4):
                do_group(g)

        xTv = xT.rearrange("(c p) t -> p c t", p=128)
        oTv = outT.rearrange("(c p) t -> p c t", p=128)
        for t in range(n_tiles):
            t0 = t * TT
            P.op("sp", lambda e, t0=t0: e.dma_start(out=hT[:, :, :], in_=xTv[:, :, t0:t0 + TT]), writes=HT, dma="xin")
            P.op("sp", lambda e, t0=t0: e.dma_start(out=posi[:, :], in_=pos[:, t0:t0 + TT].partition_broadcast(128)), writes=["posi"], dma="pin")
            for l in layers:
                if l % 2 == 0:
                    ret_layer(l)
                else:
                    gdn_layer(l)
                ffn(l)
            FF.reset()
            BB.reset()
            ob = FF.alloc(4096)
            if final_norm:
                rmsnorm(O_NFIN, dst_f32=ob)
            else:
                for c in range(8):
                    P.op("act", lambda e, c=c: e.activation(out=ob.sub(c * 512, 512).ap(), in_=hT[:, c, :], func=AF.Copy),
                         reads=[("hT", c)], writes=[ob.sub(c * 512, 512)])
            P.op("sp", lambda e, t0=t0: e.dma_start(out=oTv[:, :, t0:t0 + TT], in_=ob.r("p (c t) -> p c t", c=8)), reads=[ob], writes=[("out", t % 2)], dma=f"xout{t % 2}")
        P.op("sp", None, reads=[("out", 0), ("out", 1)] + [("dbg", i) for i in dcount])
        P.emit(nc, es)
    return nc


def _layer_slots(l):
    base = 0
    for i in range(l):
        base += SLOTS_RET if i % 2 == 0 else SLOTS_GDN
    n = SLOTS_RET if l % 2 == 0 else SLOTS_GDN
    return list(range(base, base + n))


N_CORES = 8
GDN_STOP = 99


def kernel(x, positions, norm_mix, norm_ffn, norm_final, ret_w_in, ret_gn_gain, ret_w_out,
           gdn_w_in, gdn_conv, gdn_a_log, gdn_dt_bias, gdn_norm_gain, gdn_w_out, ffn_w_in, ffn_w_out):
    inp = dict(norm_mix=norm_mix, norm_ffn=norm_ffn, norm_final=norm_final, ret_w_in=ret_w_in, ret_gn_gain=ret_gn_gain,
               ret_w_out=ret_w_out, gdn_w_in=gdn_w_in, gdn_conv=gdn_conv, gdn_a_log=gdn_a_log, gdn_dt_bias=gdn_dt_bias,
               gdn_norm_gain=gdn_norm_gain, gdn_w_out=gdn_w_out, ffn_w_in=ffn_w_in, ffn_w_out=ffn_w_out)
    inp = {k: np.asarray(v, np.float32) for k, v in inp.items()}
    x = np.asarray(x, np.float32)
    positions = np.asarray(positions, np.int32)
    wall = pack_weights(inp)
    prm = pack_params(inp)
    cmat = const_mats()
    nc = build_program()
    in_maps = []
    for c in range(N_CORES):
        b = c % BATCH
        in_maps.append({"xT": np.ascontiguousarray(x[b].T), "pos": np.ascontiguousarray(positions[b][None, :]),
                        "prm": prm, "cmat": cmat, "wall": wall})
    res = run_bass_kernel_spmd(nc, in_maps, core_ids=list(range(N_CORES)))
    out = np.stack([np.ascontiguousarray(res.results[b]["outT"].T) for b in range(BATCH)], 0)
    return out.astype(np.float32)
```
